# Optimizing a Trainium2 kernel written in Bass

```python
import math
import jax, jax.numpy as jnp
from jax import lax
import numpy as np

D_MODEL = 4096
BATCH = 4
SEQ = 2048
DEPTH = 1

MEM_TOKENS = 256
FFN_DIM = 256 * ((8 * D_MODEL // 3 + 255) // 256)
CONV_WIDTH = D_MODEL // 2
CONV_K = 3
DN_HEAD_DIM = 128
DN_HEADS = (D_MODEL // 2) // DN_HEAD_DIM
DN_WIDTH = DN_HEADS * DN_HEAD_DIM
DN_CONV_K = 4
DN_CHUNK = 64
XATTN_HEADS = 4
XATTN_HEAD_DIM = 128
XATTN_WIDTH = XATTN_HEADS * XATTN_HEAD_DIM
EPS = 1e-6
IN_SPLIT_SIZES = (CONV_WIDTH, CONV_WIDTH, CONV_WIDTH, 3 * DN_WIDTH, DN_WIDTH, DN_HEADS, DN_HEADS, D_MODEL, D_MODEL)
IN_PROJ_DIM = 3 * CONV_WIDTH + 4 * DN_WIDTH + 2 * DN_HEADS + 2 * D_MODEL

kernel_name = "hybrid_conv_deltanet_macaron_memxattn"


def rms_norm(x, w):
    xf = x.astype(jnp.float32)
    y = xf * lax.rsqrt(jnp.mean(xf * xf, axis=-1, keepdims=True) + EPS)
    return (y * w.astype(jnp.float32)).astype(x.dtype)


def l2_norm(x):
    xf = x.astype(jnp.float32)
    return xf * lax.rsqrt(jnp.sum(xf * xf, axis=-1, keepdims=True) + EPS)


def swiglu(x, w_gate, w_up, w_down):
    return (jax.nn.silu(x @ w_gate) * (x @ w_up)) @ w_down


def causal_dwconv(x, w):
    K = w.shape[0]
    S = x.shape[1]
    xp = jnp.pad(x, ((0, 0), (K - 1, 0), (0, 0)))
    y = xp[:, 0:S, :] * w[0]
    for i in range(1, K):
        y = y + xp[:, i:i + S, :] * w[i]
    return y


def gated_delta_rule_chunked(q, k, v, g, beta):
    Bsz, S, H, DK = q.shape
    DV = v.shape[-1]
    C = DN_CHUNK
    NC = S // C

    def chunk(t):
        t = t.astype(jnp.float32).reshape((Bsz, NC, C, H) + t.shape[3:])
        return jnp.moveaxis(t, 3, 1)

    q, k, v, g, beta = chunk(q), chunk(k), chunk(v), chunk(g), chunk(beta)
    gc = jnp.cumsum(g, axis=-1)
    idx = jnp.arange(C)
    incl = idx[:, None] >= idx[None, :]
    strict = idx[:, None] > idx[None, :]
    decay = jnp.exp(jnp.where(incl, gc[..., :, None] - gc[..., None, :], -jnp.inf))
    k_beta = k * beta[..., None]
    v_beta = v * beta[..., None]
    L = jnp.where(strict, jnp.einsum('bhnid,bhnjd->bhnij', k_beta, k) * decay, 0.0)
    rhs = jnp.concatenate([v_beta, k_beta * jnp.exp(gc)[..., None]], axis=-1)
    sol = lax.linalg.triangular_solve(L + jnp.eye(C, dtype=jnp.float32), rhs,
                                      left_side=True, lower=True, unit_diagonal=True)
    u_base, w_dec = sol[..., :DV], sol[..., DV:]
    a_qk = jnp.einsum('bhnid,bhnjd->bhnij', q, k) * decay
    q_dec = q * jnp.exp(gc)[..., None]
    k_tail = k * jnp.exp(gc[..., -1:] - gc)[..., None]
    g_last = jnp.exp(gc[..., -1])

    xs = tuple(jnp.moveaxis(t, 2, 0) for t in (u_base, w_dec, a_qk, q_dec, k_tail, g_last))

    def step(state, inp):
        ub, wc, aqk, qd, kt, gl = inp
        u = ub - jnp.einsum('bhck,bhkv->bhcv', wc, state)
        o = jnp.einsum('bhck,bhkv->bhcv', qd, state) + jnp.einsum('bhij,bhjv->bhiv', aqk, u)
        state = state * gl[..., None, None] + jnp.einsum('bhck,bhcv->bhkv', kt, u)
        return state, o

    s0 = jnp.zeros((Bsz, H, DK, DV), jnp.float32)
    _, o = lax.scan(step, s0, xs)
    o = jnp.moveaxis(o, 0, 2).reshape(Bsz, H, S, DV)
    return jnp.transpose(o, (0, 2, 1, 3))


def memory_cross_attention(hn, mn, wq, wk, wv, wo):
    Bsz, S, _ = hn.shape
    M = mn.shape[1]
    qh = (hn @ wq).reshape(Bsz, S, XATTN_HEADS, XATTN_HEAD_DIM)
    kh = (mn @ wk).reshape(Bsz, M, XATTN_HEADS, XATTN_HEAD_DIM)
    vh = (mn @ wv).reshape(Bsz, M, XATTN_HEADS, XATTN_HEAD_DIM)
    scores = jnp.einsum('bshd,bmhd->bhsm', qh.astype(jnp.float32), kh.astype(jnp.float32))
    p = jax.nn.softmax(scores * (XATTN_HEAD_DIM ** -0.5), axis=-1)
    o = jnp.einsum('bhsm,bmhd->bshd', p, vh.astype(jnp.float32)).astype(hn.dtype)
    return o.reshape(Bsz, S, XATTN_WIDTH) @ wo


def setup_inputs(seed: int = 0) -> dict:
    key = jax.random.key(seed)
    ks = jax.random.split(key, 32)

    def dense(k, shape, fan_in):
        return jax.random.normal(k, (DEPTH,) + shape, jnp.float32) * (fan_in ** -0.5)

    def gain(k, n):
        return 1.0 + 0.02 * jax.random.normal(k, (DEPTH, n), jnp.float32)

    dt = jnp.exp(jax.random.uniform(ks[14], (DEPTH, DN_HEADS), jnp.float32,
                                    math.log(1e-3), math.log(1e-1)))
    return {
        "x": jax.random.normal(ks[0], (BATCH, SEQ, D_MODEL), jnp.float32),
        "mem": jax.random.normal(ks[1], (BATCH, MEM_TOKENS, D_MODEL), jnp.float32),
        "ffn1_norm": gain(ks[2], D_MODEL),
        "ffn1_w_gate": dense(ks[3], (D_MODEL, FFN_DIM), D_MODEL),
        "ffn1_w_up": dense(ks[4], (D_MODEL, FFN_DIM), D_MODEL),
        "ffn1_w_down": dense(ks[5], (FFN_DIM, D_MODEL), FFN_DIM),
        "mix_norm": gain(ks[6], D_MODEL),
        "w_in": dense(ks[7], (D_MODEL, IN_PROJ_DIM), D_MODEL),
        "conv_w": dense(ks[8], (CONV_K, CONV_WIDTH), CONV_K),
        "qkv_conv_w": dense(ks[9], (DN_CONV_K, 3 * DN_WIDTH), DN_CONV_K),
        "a_log": jnp.log(jax.random.uniform(ks[10], (DEPTH, DN_HEADS), jnp.float32, 1.0, 16.0)),
        "dt_bias": dt + jnp.log(-jnp.expm1(-dt)),
        "dn_out_norm": gain(ks[11], DN_HEAD_DIM),
        "w_out_conv": dense(ks[12], (CONV_WIDTH, D_MODEL), CONV_WIDTH),
        "w_out_delta": dense(ks[13], (DN_WIDTH, D_MODEL), DN_WIDTH),
        "w_o": dense(ks[15], (D_MODEL, D_MODEL), D_MODEL),
        "xattn_norm": gain(ks[16], D_MODEL),
        "mem_norm": gain(ks[17], D_MODEL),
        "xattn_wq": dense(ks[18], (D_MODEL, XATTN_WIDTH), D_MODEL),
        "xattn_wk": dense(ks[19], (D_MODEL, XATTN_WIDTH), D_MODEL),
        "xattn_wv": dense(ks[20], (D_MODEL, XATTN_WIDTH), D_MODEL),
        "xattn_wo": dense(ks[21], (XATTN_WIDTH, D_MODEL), XATTN_WIDTH),
        "ffn2_norm": gain(ks[22], D_MODEL),
        "ffn2_w_gate": dense(ks[23], (D_MODEL, FFN_DIM), D_MODEL),
        "ffn2_w_up": dense(ks[24], (D_MODEL, FFN_DIM), D_MODEL),
        "ffn2_w_down": dense(ks[25], (FFN_DIM, D_MODEL), FFN_DIM),
        "final_norm": 1.0 + 0.02 * jax.random.normal(ks[26], (D_MODEL,), jnp.float32),
    }


def reference(x, mem, ffn1_norm, ffn1_w_gate, ffn1_w_up, ffn1_w_down, mix_norm, w_in,
              conv_w, qkv_conv_w, a_log, dt_bias, dn_out_norm, w_out_conv, w_out_delta, w_o,
              xattn_norm, mem_norm, xattn_wq, xattn_wk, xattn_wv, xattn_wo,
              ffn2_norm, ffn2_w_gate, ffn2_w_up, ffn2_w_down, final_norm):
    Bsz, S, _ = x.shape
    split_points = [int(v) for v in np.cumsum(IN_SPLIT_SIZES)[:-1]]
    h = x
    for l in range(DEPTH):
        h = h + 0.5 * swiglu(rms_norm(h, ffn1_norm[l]), ffn1_w_gate[l], ffn1_w_up[l], ffn1_w_down[l])

        u = rms_norm(h, mix_norm[l])
        proj = u @ w_in[l]
        c_x, c_c, c_b, qkv, z, b_logit, a_logit, gate_a, gate_b = jnp.split(proj, split_points, axis=-1)

        y_conv = (c_b * causal_dwconv(c_c * c_x, conv_w[l])) @ w_out_conv[l]

        qkv = jax.nn.silu(causal_dwconv(qkv, qkv_conv_w[l]))
        q, k, v = jnp.split(qkv, 3, axis=-1)
        q = l2_norm(q.reshape(Bsz, S, DN_HEADS, DN_HEAD_DIM)) * (DN_HEAD_DIM ** -0.5)
        k = l2_norm(k.reshape(Bsz, S, DN_HEADS, DN_HEAD_DIM))
        v = v.reshape(Bsz, S, DN_HEADS, DN_HEAD_DIM)
        beta = jax.nn.sigmoid(b_logit.astype(jnp.float32))
        g = -jnp.exp(a_log[l].astype(jnp.float32)) * jax.nn.softplus(
            a_logit.astype(jnp.float32) + dt_bias[l].astype(jnp.float32))
        o = gated_delta_rule_chunked(q, k, v, g, beta)
        zf = z.astype(jnp.float32).reshape(Bsz, S, DN_HEADS, DN_HEAD_DIM)
        o = o * lax.rsqrt(jnp.mean(o * o, axis=-1, keepdims=True) + EPS) \
            * dn_out_norm[l].astype(jnp.float32) * jax.nn.silu(zf)
        y_delta = o.reshape(Bsz, S, DN_WIDTH).astype(x.dtype) @ w_out_delta[l]

        merged = jax.nn.sigmoid(gate_a) * y_conv + jax.nn.sigmoid(gate_b) * y_delta
        h = h + merged @ w_o[l]

        h = h + memory_cross_attention(rms_norm(h, xattn_norm[l]), rms_norm(mem, mem_norm[l]),
                                       xattn_wq[l], xattn_wk[l], xattn_wv[l], xattn_wo[l])

        h = h + 0.5 * swiglu(rms_norm(h, ffn2_norm[l]), ffn2_w_gate[l], ffn2_w_up[l], ffn2_w_down[l])
    return rms_norm(h, final_norm)
```

```python
import numpy as np
import concourse.bass as bass
import concourse.mybir as mybir

F32 = mybir.dt.float32
BF16 = mybir.dt.bfloat16
AF = mybir.ActivationFunctionType
ALU = mybir.AluOpType

ENGS = ["tensor", "vector", "scalar", "gpsimd", "sync"]


class Buf:
    __slots__ = ("name", "writer", "readers", "excl")

    def __init__(self, name, excl=False):
        self.name = name
        self.writer = None
        self.readers = {}
        self.excl = excl


class Op:
    __slots__ = ("eng", "fn", "deps", "signal", "sem", "val", "kind", "epoch")

    def __init__(self, eng, fn, kind):
        self.eng = eng
        self.fn = fn
        self.kind = kind
        self.deps = []
        self.signal = False
        self.sem = None
        self.val = 0
        self.epoch = 0


class Sched:
    def __init__(self, nc, ndma=12):
        self.nc = nc
        self.ops = {e: [] for e in ENGS}
        self.sems = {e: nc.semaphore("sem_" + e).__enter__() for e in ENGS}
        self.cnt = {e: 0 for e in ENGS}
        self.dsems = {e: [nc.semaphore("dsem_%s_%d" % (e, i)).__enter__() for i in range(ndma)]
                      for e in ("gpsimd", "sync", "scalar")}
        self.dcnt = {e: [0] * ndma for e in ("gpsimd", "sync", "scalar")}
        self.drr = {"gpsimd": 0, "sync": 0, "scalar": 0}
        self.waited = {e: {} for e in ENGS}
        self.pending = []
        self.alldma = []
        self.last = {e: None for e in ENGS}
        self.ccs = []
        self.epoch = 0
        self.nsem = 0

    def add(self, eng, fn, reads=(), writes=(), kind="c"):
        op = Op(eng, fn, kind)
        op.epoch = self.epoch
        deps = []

        def dep(o):
            if o is None or o is op or (o.epoch < self.epoch and o.kind != "cc"):
                return
            if o.eng == eng and o.kind == "c" and kind == "c" and eng == "tensor":
                return
            deps.append(o)

        for b in reads:
            if b.excl:
                if b.writer is not None and b.writer.eng != eng:
                    dep(b.writer)
                for e2, o in b.readers.items():
                    if e2 != eng:
                        dep(o)
            else:
                dep(b.writer)
        for b in writes:
            if b.excl:
                if b.writer is not None and b.writer.eng != eng:
                    dep(b.writer)
                for e2, o in b.readers.items():
                    if e2 != eng:
                        dep(o)
            else:
                dep(b.writer)
                for o in b.readers.values():
                    dep(o)
        for b in reads:
            if kind in ("d", "cc"):
                b.readers[("dma", id(op))] = op
            else:
                b.readers[eng] = op
        for b in writes:
            b.writer = op
            b.readers = {}
        seen = set()
        for d in deps:
            if id(d) not in seen:
                seen.add(id(d))
                d.signal = True
                op.deps.append(d)
        self.ops[eng].append(op)
        self.pending.append(op)
        if kind in ("d", "cc"):
            self.alldma.append(op)
        self.last[eng] = op
        return op

    def barrier(self):
        lasts = [o for o in self.last.values() if o is not None and o.kind == "c"]
        dmas = [d for d in self.alldma if d.kind != "cc"]
        self.alldma = []
        for e in ENGS:
            op = Op(e, None, "w")
            op.epoch = self.epoch
            for d in lasts + dmas:
                if d.eng == e and d.kind == "c":
                    continue
                d.signal = True
                op.deps.append(d)
            self.ops[e].append(op)
            self.pending.append(op)

    def emit(self, block):
        nc = self.nc
        self.barrier()
        self.epoch += 1
        pend = self.pending
        self.pending = []
        for e in ENGS:
            if self.cnt[e] > 1500:
                self.nsem += 1
                self.sems[e] = nc.semaphore("sem_%s_%d" % (e, self.nsem)).__enter__()
                self.cnt[e] = 0
        for op in pend:
            if op.kind == "c":
                if op.signal:
                    self.cnt[op.eng] += 1
                    op.sem = self.sems[op.eng]
                    op.val = self.cnt[op.eng]
            elif op.kind == "d":
                i = self.drr[op.eng]
                self.drr[op.eng] = (i + 1) % len(self.dsems[op.eng])
                self.dcnt[op.eng][i] += 16
                op.sem = self.dsems[op.eng][i]
                op.val = self.dcnt[op.eng][i]
            elif op.kind == "cc":
                s = nc.semaphore("ccsem%d" % len(self.ccs)).__enter__()
                self.ccs.append(s)
                op.sem = s
                op.val = 1
        per = {e: [o for o in pend if o.eng == e] for e in ENGS}

        def make(e):
            def body(eng):
                w = self.waited[e]
                for op in per[e]:
                    for d in op.deps:
                        k = id(d.sem)
                        if w.get(k, 0) < d.val:
                            eng.wait_ge(d.sem, d.val)
                            w[k] = d.val
                    if op.fn is None:
                        continue
                    ins = op.fn(eng)
                    if op.kind == "d":
                        ins.then_inc(op.sem, 16)
                    elif op.kind == "cc":
                        ins.then_inc(op.sem, 1)
                    elif op.signal:
                        ins.then_inc(op.sem, 1)
            return body

        for e in ENGS:
            if per[e]:
                getattr(block, e)(make(e))

from concourse.bass_utils import run_bass_kernel_spmd

D = 4096
FF = 11008
NTOK = 1024
T = 512
NMT = NTOK // T
NH = 16
EPS = 1e-6
HG = [(0, 22), (22, 44), (44, 65), (65, 86)]
NSLOT = 6
PSZ = 4

GROUPS = {
    "f1gu": (176, 32), "f1d": (128, 22), "in1": (88, 32), "in4": (96, 32), "oc": (32, 16),
    "od": (32, 16), "wo": (32, 32), "xa": (16, 32), "xo": (32, 4), "f2gu": (176, 32), "f2d": (128, 22),
}
GORDER = ["f1gu", "f1d", "in1", "in4", "oc", "od", "wo", "xa", "xo", "f2gu", "f2d"]

V_GAIN = 0
V_CA = 192
V_CQ = 240
V_ALOG = 432
V_DTB = 448
V_DNG = 464
V_OH = 468
NV = 476
C_I, C_ONE, C_NEG, C_TRI, C_UPP, C_NMS, C_NMT = 0, 128, 256, 384, 448, 512, 576
NCC = 640


def _mkchunk(Wsub, kc):
    K, n = Wsub.shape
    out = np.zeros((128, kc, 128), np.float32)
    k = K // 128
    out[:, :k, :n] = Wsub.reshape(k, 128, n).transpose(1, 0, 2)
    return out


def _prep_weights(inp):
    g = {}
    def ffn(pre, wg, wu, wd):
        gu = []
        for j in range(86):
            gu.append(_mkchunk(wg[:, j * 128:(j + 1) * 128], 32))
            gu.append(_mkchunk(wu[:, j * 128:(j + 1) * 128], 32))
        g[pre + "gu"] = gu
        dd = []
        for (a, b) in HG:
            for oc in range(32):
                dd.append(_mkchunk(wd[a * 128:b * 128, oc * 128:(oc + 1) * 128], 22))
        g[pre + "d"] = dd
    ffn("f1", inp["ffn1_w_gate"][0], inp["ffn1_w_up"][0], inp["ffn1_w_down"][0])
    ffn("f2", inp["ffn2_w_gate"][0], inp["ffn2_w_up"][0], inp["ffn2_w_down"][0])
    wi = inp["w_in"][0]
    c1 = []
    for base in (6144, 8192, 10240):
        for i in range(16):
            c1.append(_mkchunk(wi[:, base + 128 * i: base + 128 * (i + 1)], 32))
    c1.append(_mkchunk(wi[:, 14336:14368], 32))
    for i in range(16):
        c1.append(_mkchunk(wi[:, 128 * i:128 * (i + 1)], 32))
        c1.append(_mkchunk(wi[:, 2048 + 128 * i:2048 + 128 * (i + 1)], 32))
    g["in1"] = c1
    c4 = []
    for base, n in ((4096, 16), (12288, 16), (14368, 32), (18464, 32)):
        for i in range(n):
            c4.append(_mkchunk(wi[:, base + 128 * i: base + 128 * (i + 1)], 32))
    g["in4"] = c4
    g["oc"] = [_mkchunk(inp["w_out_conv"][0][:, 128 * i:128 * (i + 1)], 16) for i in range(32)]
    g["od"] = [_mkchunk(inp["w_out_delta"][0][:, 128 * i:128 * (i + 1)], 16) for i in range(32)]
    g["wo"] = [_mkchunk(inp["w_o"][0][:, 128 * i:128 * (i + 1)], 32) for i in range(32)]
    g["xa"] = [_mkchunk(inp[k][0][:, 128 * i:128 * (i + 1)], 32)
               for k in ("xattn_wq", "xattn_wk", "xattn_wv") for i in range(4)]
    g["xo"] = [_mkchunk(inp["xattn_wo"][0][:, 128 * i:128 * (i + 1)], 4) for i in range(32)]
    shards = [dict() for _ in range(8)]
    for name, (n, kc) in GROUPS.items():
        ch = g[name]
        while len(ch) < n:
            ch.append(np.zeros((128, kc, 128), np.float32))
        for r in range(8):
            arr = np.stack([ch[l * 8 + r] for l in range(n // 8)], 0)
            shards[r]["w_" + name] = np.ascontiguousarray(arr.reshape(n // 8 * 128, kc * 128))
    return shards


def _prep_small(inp):
    vecs = np.zeros((128, NV), np.float32)
    for gi, k in enumerate(["ffn1_norm", "mix_norm", "xattn_norm", "mem_norm", "ffn2_norm", "final_norm"]):
        v = np.asarray(inp[k]).reshape(-1)
        vecs[:, V_GAIN + gi * 32:V_GAIN + (gi + 1) * 32] = v.reshape(32, 128).T
    cw = inp["conv_w"][0]
    vecs[:, V_CA:V_CA + 48] = cw.reshape(3, 16, 128).transpose(2, 1, 0).reshape(128, 48)
    qw = inp["qkv_conv_w"][0]
    vecs[:, V_CQ:V_CQ + 192] = qw.reshape(4, 48, 128).transpose(2, 1, 0).reshape(128, 192)
    vecs[:, V_ALOG:V_ALOG + 16] = inp["a_log"][0][None, :]
    vecs[:, V_DTB:V_DTB + 16] = inp["dt_bias"][0][None, :]
    vecs[:, V_DNG] = inp["dn_out_norm"][0]
    c = np.zeros((128, NCC), np.float32)
    c[:, C_I:C_I + 128] = np.eye(128)
    c[:, C_ONE:C_ONE + 128] = 1.0
    c[:, C_NEG:C_NEG + 128] = -1.0
    t = np.arange(64)
    c[:64, C_TRI:C_TRI + 64] = (t[:, None] <= t[None, :])
    c[:64, C_UPP:C_UPP + 64] = (t[:, None] > t[None, :])
    c[:64, C_NMS:C_NMS + 64] = np.where(t[:, None] > t[None, :], 0.0, -30000.0)
    c[:64, C_NMT:C_NMT + 64] = np.where(t[None, :] >= t[:, None], 0.0, -30000.0)
    return vecs, c


def build(debug=False, stages=("M", "1", "2", "x", "4a", "4b"), nmt=NMT, groups=None, ntok=NTOK):
    groups = list(GORDER) if groups is None else groups
    nc = bass.Bass("TRN2", target_bir_lowering=False)
    dk = "ExternalOutput" if debug else "Internal"
    xT_d = nc.dram_tensor("xT", [128, 32, ntok], F32, kind="ExternalInput").ap()
    memT_d = nc.dram_tensor("memT", [128, 32, 256], F32, kind="ExternalInput").ap()
    vecs_d = nc.dram_tensor("vecs", [128, NV], F32, kind="ExternalInput").ap()
    consts_d = nc.dram_tensor("consts", [128, NCC], F32, kind="ExternalInput").ap()
    yT_d = nc.dram_tensor("yT", [128, 32, ntok], F32, kind="ExternalOutput").ap()
    w_ext, w_cc, w_all = {}, {}, {}
    for name in groups:
        n, kc = GROUPS[name]
        w_ext[name] = nc.dram_tensor("w_" + name, [n // 8 * 128, kc * 128], F32, kind="ExternalInput").ap()
        w_cc[name] = nc.dram_tensor("cc_" + name, [n // 8 * 128, kc * 128], BF16, kind="Internal").ap()
        w_all[name] = []
        for p0 in range(0, n // 8, PSZ):
            psz = min(PSZ, n // 8 - p0)
            w_all[name].append(nc.dram_tensor("wall_%s_%d" % (name, p0), [8 * psz * 128, kc * 128], BF16, kind="Internal").ap())
    H1 = nc.dram_tensor("H1", [128, 32, ntok], F32, kind=dk).ap()
    U1 = nc.dram_tensor("U1", [128, 32, ntok], BF16, kind="Internal").ap()
    DBG = nc.dram_tensor("DBG", [128, 4096], F32, kind=dk).ap()
    QKV = nc.dram_tensor("QKV", [48, 128, 3 + ntok], F32, kind=dk).ap()
    BAp = nc.dram_tensor("BAp", [32, ntok], F32, kind=dk).ap()
    PCC = nc.dram_tensor("PCC", [16, 128, 2 + ntok], F32, kind=dk).ap()
    YB = nc.dram_tensor("YB", [16, 128, ntok], F32, kind=dk).ap()

    S = Sched(nc)
    ctx = []

    uniq = [0]

    def sbt(name, shape, dt):
        uniq[0] += 1
        return nc.sbuf_tensor("%s_%d" % (name, uniq[0]), shape, dt)

    pers = [
        nc.sbuf_tensor("vecs_s", [128, NV], F32), nc.sbuf_tensor("consts_s", [128, NCC], F32),
        nc.sbuf_tensor("ibf", [128, 128], BF16), nc.sbuf_tensor("Sst", [128, NH, 128], F32),
        nc.sbuf_tensor("negA", [128, 16], F32), nc.sbuf_tensor("KmT", [128, 4, 256], BF16),
        nc.sbuf_tensor("Vm", [128, 2, 512], BF16), nc.sbuf_tensor("zero", [128, 64], F32),
    ]
    vecs, cst, ibf, Sst, negA, KmT, Vm, zero = [p.__enter__() for p in pers]
    psum = [nc.psum_tensor("ps%d" % i, [128, 512], F32).__enter__() for i in range(8)]
    PSB = [Buf("ps%d" % i, True) for i in range(8)]
    pstate = [0]

    def bank():
        i = pstate[0]
        pstate[0] = (i + 1) % 8
        return psum[i], PSB[i]

    VEC, CST, IBF, SST, NEGA, KMT, VM, ZERO = [Buf(n) for n in ("vecs", "cst", "ibf", "sst", "nega", "kmt", "vm", "zero")]
    WALL = {g: [Buf("wall_%s_%d" % (g, p)) for p in range((GROUPS[g][0] // 8 + PSZ - 1) // PSZ)] for g in GROUPS}
    if debug:
        for _n in ("U1",):
            pass
    B_H1 = [Buf("H1_%d" % m) for m in range(NMT)]
    B_U1 = [Buf("U1_%d" % m) for m in range(NMT)]
    B_QKV, B_BA, B_PCC, B_YB, B_OUT = Buf("qkv"), Buf("ba"), Buf("pcc"), Buf("yb"), Buf("out")

    def I128():
        return cst[:, C_I:C_I + 128]

    def gain(gi, kc):
        return vecs[:, V_GAIN + gi * 32 + kc:V_GAIN + gi * 32 + kc + 1]

    def mm(o, l, r, st=True, sp=True):
        return lambda e: e.matmul(o, l, r, start=st, stop=sp)

    def act(o, i, f, **kw):
        return lambda e: e.activation(o, i, f, **kw)

    def dma(o, i):
        return lambda e: e.dma_start(out=o, in_=i)

    def tt(o, a, b, op):
        return lambda e: e.tensor_tensor(o, a, b, op)

    def stt(o, a, s, b, op0, op1):
        return lambda e: e.scalar_tensor_tensor(out=o, in0=a, scalar=s, in1=b, op0=op0, op1=op1)

    def ts(o, a, s1, s2, op0, op1=None):
        if op1 is None:
            return lambda e: e.tensor_scalar(o, a, s1, None, op0)
        return lambda e: e.tensor_scalar(o, a, s1, s2, op0, op1)

    def cp(o, i):
        return lambda e: e.tensor_copy(o, i)

    def recip(o, i):
        return lambda e: e.reciprocal(o, i)

    with nc.Block() as block:
        S.add("sync", dma(vecs[:], vecs_d), writes=[VEC], kind="d")
        S.add("sync", dma(cst[:], consts_d), writes=[CST], kind="d")
        S.add("vector", cp(ibf[:], cst[:, C_I:C_I + 128]), reads=[CST], writes=[IBF])
        S.add("vector", lambda e: e.memset(Sst[:], 0.0), writes=[SST])
        S.add("vector", lambda e: e.memset(zero[:], 0.0), writes=[ZERO])
        S.add("scalar", act(negA[:], vecs[:, V_ALOG:V_ALOG + 16], AF.Exp), reads=[VEC], writes=[NEGA])
        S.add("vector", ts(negA[:], negA[:], -1.0, None, ALU.mult), reads=[NEGA], writes=[NEGA])
        for c in range(48):
            S.add("sync", dma(QKV[c, :, 0:3], zero[:, 0:3]), reads=[ZERO], writes=[B_QKV], kind="d")
        for c in range(16):
            S.add("sync", dma(PCC[c, :, 0:2], zero[:, 0:2]), reads=[ZERO], writes=[B_PCC], kind="d")
        for g in groups:
            n, kc = GROUPS[g]
            for pi, p0 in enumerate(range(0, n // 8, PSZ)):
                psz = min(PSZ, n // 8 - p0)
                CCB = Buf("cc_%s_%d" % (g, pi))
                for l in range(p0, p0 + psz):
                    S.add("gpsimd", dma(w_cc[g][l * 128:(l + 1) * 128, :], w_ext[g][l * 128:(l + 1) * 128, :]), writes=[Buf("t")], reads=[], kind="d")
                    CCB.readers = {}
                lastd = S.ops["gpsimd"][-psz:]
                op = S.add("gpsimd", (lambda g=g, pi=pi, p0=p0, psz=psz: lambda e: e.collective_compute(
                    "AllGather", ALU.bypass, replica_groups=[list(range(8))],
                    ins=[w_cc[g][p0 * 128:(p0 + psz) * 128, :]], outs=[w_all[g][pi]]))(),
                    reads=[], writes=[WALL[g][pi]], kind="cc")
                for d_ in lastd:
                    d_.signal = True
                    op.deps.append(d_)
        S.emit(block)

    class Ring:
        def __init__(self, t, n, name):
            self.t, self.n, self.i = t, n, 0
            self.b = [Buf("%s%d" % (name, k)) for k in range(n)]

        def next(self):
            i = self.i
            self.i = (i + 1) % self.n
            return i

    def load_w(ring, g, idx):
        n, kc = GROUPS[g]
        l = idx // 8
        pi = l // PSZ
        psz = min(PSZ, n // 8 - pi * PSZ)
        row = (idx % 8) * psz + (l - pi * PSZ)
        s = ring.next()
        S.add("sync", dma(ring.t[:, s, 0:kc, :], w_all[g][pi][row * 128:(row + 1) * 128, :].rearrange("p (k n) -> p k n", n=128)),
              reads=[WALL[g][pi]], writes=[ring.b[s]], kind="d")
        return s

    def linear(ring, g, idx, rhs, rhsb, W, M=128, nk=None):
        kc_n = GROUPS[g][1] if nk is None else nk
        s = load_w(ring, g, idx)
        ps, PB = bank()
        for kc in range(kc_n):
            S.add("tensor", mm(ps[0:M, 0:W], ring.t[:, s, kc, 0:M], rhs(kc), kc == 0, kc == kc_n - 1),
                  reads=[ring.b[s], rhsb(kc)], writes=[PB] if kc in (0, kc_n - 1) else [])
        return ps, PB

    def rmsnorm(src, srcb, gi, dst, dstb, W, sq, rb, RBB, nk=32):
        ps, PB = bank()
        for kc in range(nk):
            i = sq.next()
            S.add("scalar", act(sq.t[:, i, 0:W], src(kc), AF.Square), reads=[srcb(kc)], writes=[sq.b[i]])
            S.add("tensor", mm(ps[:, 0:W], cst[:, C_ONE:C_ONE + 128], sq.t[:, i, 0:W], kc == 0, kc == nk - 1),
                  reads=[sq.b[i], CST], writes=[PB] if kc in (0, nk - 1) else [])
        S.add("scalar", act(rb[:, 0:W], ps[:, 0:W], AF.Sqrt, bias=EPS, scale=1.0 / D), reads=[PB], writes=[RBB])
        S.add("vector", recip(rb[:, 0:W], rb[:, 0:W]), reads=[RBB], writes=[RBB])
        for kc in range(nk):
            S.add("vector", stt(dst(kc), src(kc), gain(gi, kc), rb[:, 0:W], ALU.mult, ALU.mult),
                  reads=[srcb(kc), RBB, VEC], writes=[dstb(kc)])

    def ffn(ring, pre, xn, XNB, hT, HTB, hid, HIDB, sg):
        for gi_, (a, b) in enumerate(HG):
            for j in range(a, b):
                pg, PG = linear(ring, pre + "gu", 2 * j, lambda kc: xn[:, kc, :], lambda kc: XNB[kc], T)
                pu, PU = linear(ring, pre + "gu", 2 * j + 1, lambda kc: xn[:, kc, :], lambda kc: XNB[kc], T)
                i = sg.next()
                S.add("scalar", act(sg.t[:, i, :], pg[:, 0:T], AF.Silu), reads=[PG], writes=[sg.b[i]])
                S.add("vector", tt(hid[:, j - a, :], sg.t[:, i, :], pu[:, 0:T], ALU.mult),
                      reads=[sg.b[i], PU], writes=[HIDB[j - a]])
            for oc in range(32):
                pd, PD = linear(ring, pre + "d", gi_ * 32 + oc, lambda kc: hid[:, kc, :], lambda kc: HIDB[kc], T, nk=b - a)
                S.add("vector", stt(hT[:, oc, :], pd[:, 0:T], 0.5, hT[:, oc, :], ALU.mult, ALU.add),
                      reads=[PD, HTB[oc]], writes=[HTB[oc]])

    def stage1(mt):
        c0 = mt * T
        with (sbt("s1_hT", [128, 32, T], F32) as hT, sbt("s1_xn", [128, 32, T], BF16) as xn,
              sbt("s1_hid", [128, 22, T], BF16) as hid, sbt("s1_ring", [128, NSLOT, 32, 128], BF16) as ringt,
              sbt("s1_sq", [128, 2, T], F32) as sqt, sbt("s1_sg", [128, 2, T], F32) as sgt,
              sbt("s1_rb", [128, T], F32) as rb, sbt("s1_stg", [128, 4, T], F32) as stgt,
              nc.Block() as block):
            HTB = [Buf("hT%d" % k) for k in range(32)]
            XNB = [Buf("xn%d" % k) for k in range(32)]
            HIDB = [Buf("hid%d" % k) for k in range(22)]
            RBB = Buf("rb")
            ring = Ring(ringt, NSLOT, "ring")
            sq, sg, stg = Ring(sqt, 2, "sq"), Ring(sgt, 2, "sg"), Ring(stgt, 4, "stg")
            for kc in range(32):
                S.add("sync", dma(hT[:, kc, :], xT_d[:, kc, c0:c0 + T]), writes=[HTB[kc]], kind="d")
            rmsnorm(lambda kc: hT[:, kc, :], lambda kc: HTB[kc], 0, lambda kc: xn[:, kc, :], lambda kc: XNB[kc], T, sq, rb, RBB)
            import os
            lvl = int(os.environ.get("DBG_S1", "9"))
            if lvl >= 2:
                ffn(ring, "f1", xn, XNB, hT, HTB, hid, HIDB, sg)
            rmsnorm(lambda kc: hT[:, kc, :], lambda kc: HTB[kc], 1, lambda kc: xn[:, kc, :], lambda kc: XNB[kc], T, sq, rb, RBB)
            for kc in range(32):
                S.add("scalar", dma(H1[:, kc, c0:c0 + T], hT[:, kc, :]), reads=[HTB[kc]], writes=[Buf("t")], kind="d")
                S.add("scalar", dma(U1[:, kc, c0:c0 + T], xn[:, kc, :]), reads=[XNB[kc]], writes=[Buf("t")], kind="d")
            xr, xb = (lambda kc: xn[:, kc, :]), (lambda kc: XNB[kc])
            if lvl < 3:
                S.emit(block)
                return
            for c in range(48):
                ps, PB = linear(ring, "in1", c, xr, xb, T)
                i = stg.next()
                S.add("scalar", act(stg.t[:, i, :], ps[:, 0:T], AF.Copy), reads=[PB], writes=[stg.b[i]])
                S.add("scalar", dma(QKV[c, :, 3 + c0:3 + c0 + T], stg.t[:, i, :]), reads=[stg.b[i]], writes=[Buf("t")], kind="d")
            ps, PB = linear(ring, "in1", 48, xr, xb, T, M=32)
            i = stg.next()
            S.add("scalar", act(stg.t[0:32, i, :], ps[0:32, 0:T], AF.Copy), reads=[PB], writes=[stg.b[i]])
            S.add("scalar", dma(BAp[:, c0:c0 + T], stg.t[0:32, i, :]), reads=[stg.b[i]], writes=[Buf("t")], kind="d")
            for k in range(16):
                ps, PB = linear(ring, "in1", 49 + 2 * k, xr, xb, T)
                i = stg.next()
                S.add("scalar", act(stg.t[:, i, :], ps[:, 0:T], AF.Copy), reads=[PB], writes=[stg.b[i]])
                ps2, PB2 = linear(ring, "in1", 50 + 2 * k, xr, xb, T)
                i2 = stg.next()
                S.add("vector", tt(stg.t[:, i2, :], stg.t[:, i, :], ps2[:, 0:T], ALU.mult), reads=[stg.b[i], PB2], writes=[stg.b[i2]])
                S.add("scalar", dma(PCC[k, :, 2 + c0:2 + c0 + T], stg.t[:, i2, :]), reads=[stg.b[i2]], writes=[Buf("t")], kind="d")
            S.emit(block)

    def stage2(mt):
        c0 = mt * T
        with (sbt("s2_baT", [32, T], F32) as baT, sbt("s2_tm", [128, 7, 8, 16], F32) as tm,
              sbt("s2_pre", [128, 4, 3, 3 + T], F32) as pre, sbt("s2_qk", [128, 4, 3, T], F32) as qk,
              sbt("s2_cv", [128, T], F32) as cv, sbt("s2_sqb", [128, T], F32) as sqb,
              sbt("s2_rb", [128, T], F32) as rb, sbt("s2_yb", [128, 4, T], F32) as yb,
              sbt("s2_t64", [64, 4, 10, 64], F32) as t64, sbt("s2_t64w", [64, 4, 6, 128], F32) as t64w,
              sbt("s2_t128", [128, 4, 3, 64], F32) as t128, sbt("s2_ssq", [64, 4, 2], F32) as ssq,
              nc.Block() as block):
            BAT, TM, CV, SQB, RBB = Buf("bat"), Buf("tm"), Buf("cv"), Buf("sqb"), Buf("rb2")
            PREB = [Buf("pre%d" % i) for i in range(4)]
            HB = [Buf("hb%d" % i) for i in range(4)]
            YBB = [Buf("yb%d" % i) for i in range(4)]
            N64 = ["Gtri", "decs", "decT", "A0", "A1", "B0", "B1", "X0", "X1", "AQT"]
            N64W = ["KBG", "KT", "VB", "UB", "U", "ON"]
            N128 = ["egcb", "QDT", "WDT"]
            UB_ = [{n: Buf(n + str(s)) for n in N64 + N64W + N128 + ["ssq"]} for s in range(4)]
            G_, BETA, EGC, EKT, BEG, NB, EGL = range(7)

            def tmv(k, c, p=64):
                return tm[0:p, k, c, :]

            ONE64 = cst[0:64, C_ONE:C_ONE + 64]
            NEG64 = cst[0:64, C_NEG:C_NEG + 64]
            I64 = cst[0:64, C_I:C_I + 64]
            S.add("sync", dma(baT[:, :], BAp[:, c0:c0 + T]), writes=[BAT], kind="d")
            for c in range(8):
                ps, PB = bank()
                S.add("tensor", mm(ps[0:64, 0:32], baT[0:32, c * 64:(c + 1) * 64], cst[0:32, C_I:C_I + 32]), reads=[BAT, CST], writes=[PB])
                S.add("scalar", act(tmv(BETA, c), ps[0:64, 0:16], AF.Sigmoid), reads=[PB], writes=[TM])
                S.add("vector", tt(tmv(G_, c), ps[0:64, 16:32], vecs[0:64, V_DTB:V_DTB + 16], ALU.add), reads=[PB, VEC, TM], writes=[TM])
                S.add("scalar", act(tmv(G_, c), tmv(G_, c), AF.Exp), reads=[TM], writes=[TM])
                S.add("scalar", act(tmv(G_, c), tmv(G_, c), AF.Ln, bias=1.0), reads=[TM], writes=[TM])
                S.add("vector", tt(tmv(G_, c), tmv(G_, c), negA[0:64, :], ALU.mult), reads=[TM, NEGA], writes=[TM])
                ps2, PB2 = bank()
                S.add("tensor", mm(ps2[0:64, 0:16], cst[0:64, C_TRI:C_TRI + 64], tmv(G_, c)), reads=[TM, CST], writes=[PB2])
                S.add("tensor", mm(ps2[0:64, 16:32], cst[0:64, C_UPP:C_UPP + 64], tmv(G_, c)), reads=[TM, CST])
                S.add("tensor", mm(ps2[:, 32:48], cst[0:64, C_ONE:C_ONE + 128], tmv(G_, c)), reads=[TM, CST], writes=[PB2])
                S.add("scalar", act(tmv(EGC, c), ps2[0:64, 0:16], AF.Exp), reads=[PB2, TM], writes=[TM])
                S.add("scalar", act(tmv(EKT, c), ps2[0:64, 16:32], AF.Exp), reads=[PB2, TM], writes=[TM])
                S.add("scalar", act(tmv(EGL, c, 128), ps2[:, 32:48], AF.Exp), reads=[PB2, TM], writes=[TM])
                S.add("vector", tt(tmv(BEG, c), tmv(BETA, c), tmv(EGC, c), ALU.mult), reads=[TM], writes=[TM])
                S.add("vector", ts(tmv(NB, c), tmv(BETA, c), -1.0, None, ALU.mult), reads=[TM], writes=[TM])

            def prep(h, hs):
                for qi in range(3):
                    cidx = qi * 16 + h
                    S.add("sync", dma(pre[:, hs, qi, :], QKV[cidx, :, c0:c0 + 3 + T]), writes=[PREB[hs]], kind="d")
                for qi in range(3):
                    cidx = qi * 16 + h
                    wv = lambda i, cidx=cidx: vecs[:, V_CQ + cidx * 4 + i:V_CQ + cidx * 4 + i + 1]
                    S.add("vector", ts(cv[:, :], pre[:, hs, qi, 0:T], wv(0), None, ALU.mult), reads=[PREB[hs], VEC], writes=[CV])
                    for i in range(1, 4):
                        S.add("vector", stt(cv[:, :], pre[:, hs, qi, i:i + T], wv(i), cv[:, :], ALU.mult, ALU.add),
                              reads=[PREB[hs], VEC, CV], writes=[CV])
                    S.add("scalar", act(qk[:, hs, qi, :], cv[:, :], AF.Silu), reads=[CV], writes=[HB[hs]])
                for qi in (0, 1):
                    S.add("scalar", act(sqb[:, :], qk[:, hs, qi, :], AF.Square), reads=[HB[hs]], writes=[SQB])
                    ps, PB = bank()
                    S.add("tensor", mm(ps[:, 0:T], cst[:, C_ONE:C_ONE + 128], sqb[:, :]), reads=[SQB, CST], writes=[PB])
                    S.add("scalar", act(rb[:, :], ps[:, 0:T], AF.Sqrt, bias=EPS, scale=1.0), reads=[PB], writes=[RBB])
                    S.add("vector", recip(rb[:, :], rb[:, :]), reads=[RBB], writes=[RBB])
                    S.add("vector", stt(qk[:, hs, qi, :], qk[:, hs, qi, :], (128.0 ** -0.5) if qi == 0 else 1.0, rb[:, :], ALU.mult, ALU.mult),
                          reads=[HB[hs], RBB], writes=[HB[hs]])
            def unit(h, c, hs, us):
                if True:
                    ub = UB_[us]
                    ub = UB_[us]
                    cs = slice(c * 64, (c + 1) * 64)
                    qT, kT, vT = qk[:, hs, 0, cs], qk[:, hs, 1, cs], qk[:, hs, 2, cs]
                    a64 = lambda n: t64[:, us, N64.index(n), :]
                    a64w = lambda n: t64w[:, us, N64W.index(n), :]
                    a128 = lambda n: t128[:, us, N128.index(n), :]
                    sc = lambda k: tm[0:64, k, c, h:h + 1]
                    S.add("vector", ts(a64("Gtri"), cst[0:64, C_TRI:C_TRI + 64], sc(G_), None, ALU.mult), reads=[CST, TM], writes=[ub["Gtri"]])
                    yield
                    ps, PB = bank()
                    S.add("tensor", mm(ps[0:64, 0:64], a64("Gtri"), ONE64, True, False), reads=[ub["Gtri"], CST], writes=[PB])
                    S.add("tensor", mm(ps[0:64, 0:64], NEG64, a64("Gtri"), False, False), reads=[ub["Gtri"], CST])
                    S.add("tensor", mm(ps[0:64, 0:64], I64, cst[0:64, C_NMS:C_NMS + 64], False, True), reads=[CST], writes=[PB])
                    S.add("scalar", act(a64("decs"), ps[0:64, 0:64], AF.Exp), reads=[PB], writes=[ub["decs"]])
                    yield
                    ps, PB = bank()
                    S.add("tensor", mm(ps[0:64, 0:64], ONE64, a64("Gtri"), True, False), reads=[ub["Gtri"], CST], writes=[PB])
                    S.add("tensor", mm(ps[0:64, 0:64], a64("Gtri"), NEG64, False, False), reads=[ub["Gtri"], CST])
                    S.add("tensor", mm(ps[0:64, 0:64], I64, cst[0:64, C_NMT:C_NMT + 64], False, True), reads=[CST], writes=[PB])
                    S.add("scalar", act(a64("decT"), ps[0:64, 0:64], AF.Exp), reads=[PB], writes=[ub["decT"]])
                    yield
                    ps, PB = bank()
                    S.add("tensor", mm(ps[:, 0:64], cst[0:64, C_ONE:C_ONE + 128], a64("Gtri")), reads=[ub["Gtri"], CST], writes=[PB])
                    S.add("scalar", act(a128("egcb"), ps[:, 0:64], AF.Exp), reads=[PB], writes=[ub["egcb"]])
                    S.add("vector", tt(a128("QDT"), qT, a128("egcb"), ALU.mult), reads=[HB[hs], ub["egcb"]], writes=[ub["QDT"]])
                    yield
                    ps, PB = bank()
                    S.add("tensor", mm(ps[0:64, 0:64], kT, kT), reads=[HB[hs]], writes=[PB])
                    S.add("vector", stt(a64("A0"), ps[0:64, 0:64], sc(NB), a64("decs"), ALU.mult, ALU.mult), reads=[PB, TM, ub["decs"]], writes=[ub["A0"]])
                    yield
                    ps, PB = bank()
                    S.add("tensor", mm(ps[0:64, 0:64], kT, qT), reads=[HB[hs]], writes=[PB])
                    S.add("vector", tt(a64("AQT"), ps[0:64, 0:64], a64("decT"), ALU.mult), reads=[PB, ub["decT"]], writes=[ub["AQT"]])
                    yield
                    ps, PB = bank()
                    S.add("tensor", mm(ps[0:64, 0:64], a64("A0"), I64), reads=[ub["A0"], CST], writes=[PB])
                    S.add("scalar", act(a64("B0"), ps[0:64, 0:64], AF.Copy), reads=[PB], writes=[ub["B0"]])
                    S.add("vector", tt(a64("X0"), ps[0:64, 0:64], I64, ALU.add), reads=[PB, CST], writes=[ub["X0"]])
                    for lvl in range(1, 6):
                        cu, nx = str((lvl - 1) % 2), str(lvl % 2)
                        yield
                        ps, PB = bank()
                        S.add("tensor", mm(ps[0:64, 0:64], a64("B" + cu), a64("A" + cu)), reads=[ub["A" + cu], ub["B" + cu]], writes=[PB])
                        S.add("scalar", act(a64("A" + nx), ps[0:64, 0:64], AF.Copy), reads=[PB], writes=[ub["A" + nx]])
                        if lvl < 5:
                            yield
                            ps, PB = bank()
                            S.add("tensor", mm(ps[0:64, 0:64], a64("A" + cu), a64("B" + cu)), reads=[ub["A" + cu], ub["B" + cu]], writes=[PB])
                            S.add("vector", cp(a64("B" + nx), ps[0:64, 0:64]), reads=[PB], writes=[ub["B" + nx]])
                        yield
                        ps, PB = bank()
                        S.add("tensor", mm(ps[0:64, 0:64], a64("A" + nx), a64("X" + cu)), reads=[ub["A" + nx], ub["X" + cu]], writes=[PB])
                        S.add("vector", tt(a64("X" + nx), a64("X" + cu), ps[0:64, 0:64], ALU.add), reads=[PB, ub["X" + cu]], writes=[ub["X" + nx]])
                    XT, XB = a64("X1"), ub["X1"]
                    yield
                    ps, PB = bank()
                    S.add("tensor", mm(ps[0:64, 0:128], kT, I128()), reads=[HB[hs], CST], writes=[PB])
                    S.add("scalar", act(a64w("KBG"), ps[0:64, 0:128], AF.Copy, scale=sc(BEG)), reads=[PB, TM], writes=[ub["KBG"]])
                    S.add("vector", ts(a64w("KT"), ps[0:64, 0:128], sc(EKT), None, ALU.mult), reads=[PB, TM], writes=[ub["KT"]])
                    yield
                    ps, PB = bank()
                    S.add("tensor", mm(ps[0:64, 0:128], vT, I128()), reads=[HB[hs], CST], writes=[PB])
                    S.add("scalar", act(a64w("VB"), ps[0:64, 0:128], AF.Copy, scale=sc(BETA)), reads=[PB, TM], writes=[ub["VB"]])
                    yield
                    ps, PB = bank()
                    S.add("tensor", mm(ps[:, 0:64], a64w("KBG"), XT), reads=[ub["KBG"], XB], writes=[PB])
                    S.add("scalar", act(a128("WDT"), ps[:, 0:64], AF.Copy), reads=[PB], writes=[ub["WDT"]])
                    yield
                    ps, PB = bank()
                    S.add("tensor", mm(ps[0:64, 0:128], XT, a64w("VB")), reads=[ub["VB"], XB], writes=[PB])
                    S.add("vector", cp(a64w("UB"), ps[0:64, 0:128]), reads=[PB], writes=[ub["UB"]])
                    yield
                    ps, PB = bank()
                    S.add("tensor", mm(ps[0:64, 0:128], a128("WDT"), Sst[:, h, :]), reads=[ub["WDT"], SST], writes=[PB])
                    S.add("vector", tt(a64w("U"), a64w("UB"), ps[0:64, 0:128], ALU.subtract), reads=[PB, ub["UB"]], writes=[ub["U"]])
                    yield
                    ps, PB = bank()
                    S.add("tensor", mm(ps[0:64, 0:128], a128("QDT"), Sst[:, h, :], True, False), reads=[ub["QDT"], SST], writes=[PB])
                    S.add("tensor", mm(ps[0:64, 0:128], a64("AQT"), a64w("U"), False, True), reads=[ub["AQT"], ub["U"]], writes=[PB])
                    S.add("scalar", act(a64w("ON"), ps[0:64, 0:128], AF.Square, accum_out=ssq[:, us, 0:1]), reads=[PB], writes=[ub["ON"], ub["ssq"]])
                    S.add("scalar", act(ssq[:, us, 1:2], ssq[:, us, 0:1], AF.Sqrt, bias=EPS, scale=1.0 / 128), reads=[ub["ssq"]], writes=[ub["ssq"]])
                    S.add("vector", recip(ssq[:, us, 1:2], ssq[:, us, 1:2]), reads=[ub["ssq"]], writes=[ub["ssq"]])
                    S.add("vector", ts(a64w("ON"), ps[0:64, 0:128], ssq[:, us, 1:2], None, ALU.mult), reads=[PB, ub["ssq"], ub["ON"]], writes=[ub["ON"]])
                    yield
                    ps2, PB2 = bank()
                    S.add("tensor", mm(ps2[:, 0:128], a64w("KT"), a64w("U")), reads=[ub["KT"], ub["U"]], writes=[PB2])
                    S.add("vector", stt(Sst[:, h, :], Sst[:, h, :], tm[:, EGL, c, h:h + 1], ps2[:, 0:128], ALU.mult, ALU.add),
                          reads=[PB2, TM, SST], writes=[SST])
                    yield
                    ps, PB = bank()
                    S.add("tensor", mm(ps[:, 0:64], a64w("ON"), I64), reads=[ub["ON"], CST], writes=[PB])
                    S.add("scalar", act(yb[:, hs, cs], ps[:, 0:64], AF.Copy, scale=vecs[:, V_DNG:V_DNG + 1]), reads=[PB, VEC], writes=[YBB[hs]])


            G = 4
            for hg in range(0, NH, G):
                for h in range(hg, hg + G):
                    prep(h, h - hg)
                for c in range(8):
                    gens = [unit(h, c, h - hg, h - hg) for h in range(hg, hg + G)]
                    while gens:
                        for g_ in list(gens):
                            try:
                                next(g_)
                            except StopIteration:
                                gens.remove(g_)
                for h in range(hg, hg + G):
                    hs = h - hg
                    S.add("scalar", dma(YB[h, :, c0:c0 + T], yb[:, hs, :]), reads=[YBB[hs]], writes=[Buf("t")], kind="d")
            S.emit(block)

    def stageM():
        with (sbt("sm_mem", [128, 32, 256], F32) as memT, sbt("sm_mn", [128, 32, 256], BF16) as mn,
              sbt("sm_ring", [128, NSLOT, 32, 128], BF16) as ringt, sbt("sm_sq", [128, 2, 256], F32) as sqt,
              sbt("sm_rb", [128, 256], F32) as rb, sbt("sm_vmT", [128, 4, 256], BF16) as vmT,
              nc.Block() as block):
            MB = [Buf("mem%d" % k) for k in range(32)]
            MNB = [Buf("mn%d" % k) for k in range(32)]
            RBB, VMT = Buf("rbm"), Buf("vmT")
            ring, sq = Ring(ringt, NSLOT, "ringm"), Ring(sqt, 2, "sqm")
            for kc in range(32):
                S.add("sync", dma(memT[:, kc, :], memT_d[:, kc, :]), writes=[MB[kc]], kind="d")
            rmsnorm(lambda kc: memT[:, kc, :], lambda kc: MB[kc], 3, lambda kc: mn[:, kc, :], lambda kc: MNB[kc], 256, sq, rb, RBB)
            for c in range(4):
                ps, PB = linear(ring, "xa", 4 + c, lambda kc: mn[:, kc, :], lambda kc: MNB[kc], 256)
                S.add("scalar", act(KmT[:, c, :], ps[:, 0:256], AF.Copy), reads=[PB], writes=[KMT])
            for c in range(4):
                ps, PB = linear(ring, "xa", 8 + c, lambda kc: mn[:, kc, :], lambda kc: MNB[kc], 256)
                S.add("scalar", act(vmT[:, c, :], ps[:, 0:256], AF.Copy), reads=[PB], writes=[VMT])
            for c in range(4):
                for mc in range(2):
                    ps, PB = bank()
                    S.add("tensor", mm(ps[:, 0:128], vmT[:, c, mc * 128:(mc + 1) * 128], ibf[:, :]), reads=[VMT, IBF], writes=[PB])
                    S.add("vector", cp(Vm[:, mc, c * 128:(c + 1) * 128], ps[:, 0:128]), reads=[PB], writes=[VM])
            S.emit(block)

    def stage4a(mt):
        c0 = mt * T
        with (sbt("a_u", [128, 32, T], BF16) as uT, sbt("a_ring", [128, NSLOT, 32, 128], BF16) as ringt,
              sbt("a_yA", [128, 16, T], BF16) as yA, sbt("a_yB", [128, 16, T], BF16) as yB,
              sbt("a_mrg", [128, 32, T], BF16) as mrg, sbt("a_pcc", [128, 2, 2 + T], F32) as pcc,
              sbt("a_cv", [128, T], F32) as cv, sbt("a_ybl", [128, 2, T], F32) as ybl,
              sbt("a_sz", [128, 2, T], F32) as szt, sbt("a_m", [128, 2, T], F32) as mt_,
              sbt("a_h", [128, 2, T], F32) as hch, nc.Block() as block):
            UBF = [Buf("u%d" % k) for k in range(32)]
            YAB = [Buf("ya%d" % k) for k in range(16)]
            YBB = [Buf("yb%d" % k) for k in range(16)]
            MRB = [Buf("mr%d" % k) for k in range(32)]
            CV = Buf("cva")
            ring = Ring(ringt, NSLOT, "ringa")
            pc, yl, sz, mm_, hc = Ring(pcc, 2, "pc"), Ring(ybl, 2, "yl"), Ring(szt, 2, "sz"), Ring(mt_, 2, "m"), Ring(hch, 2, "hc")
            for kc in range(32):
                S.add("sync", dma(uT[:, kc, :], U1[:, kc, c0:c0 + T]), writes=[UBF[kc]], kind="d")
            ur, ubf = (lambda kc: uT[:, kc, :]), (lambda kc: UBF[kc])
            for i in range(16):
                ps, PB = linear(ring, "in4", i, ur, ubf, T)
                k = pc.next()
                S.add("sync", dma(pcc[:, k, :], PCC[i, :, c0:c0 + 2 + T]), writes=[pc.b[k]], kind="d")
                wv = lambda j, i=i: vecs[:, V_CA + i * 3 + j:V_CA + i * 3 + j + 1]
                S.add("vector", ts(cv[:, :], pcc[:, k, 0:T], wv(0), None, ALU.mult), reads=[pc.b[k], VEC], writes=[CV])
                for j in (1, 2):
                    S.add("vector", stt(cv[:, :], pcc[:, k, j:j + T], wv(j), cv[:, :], ALU.mult, ALU.add), reads=[pc.b[k], VEC, CV], writes=[CV])
                S.add("vector", tt(yA[:, i, :], cv[:, :], ps[:, 0:T], ALU.mult), reads=[CV, PB], writes=[YAB[i]])
            for h in range(16):
                ps, PB = linear(ring, "in4", 16 + h, ur, ubf, T)
                k = sz.next()
                S.add("scalar", act(szt[:, k, :], ps[:, 0:T], AF.Silu), reads=[PB], writes=[sz.b[k]])
                k2 = yl.next()
                S.add("sync", dma(ybl[:, k2, :], YB[h, :, c0:c0 + T]), writes=[yl.b[k2]], kind="d")
                S.add("vector", tt(yB[:, h, :], ybl[:, k2, :], szt[:, k, :], ALU.mult), reads=[yl.b[k2], sz.b[k]], writes=[YBB[h]])
            for oc in range(32):
                ps, PB = linear(ring, "in4", 32 + oc, ur, ubf, T)
                k = sz.next()
                S.add("scalar", act(szt[:, k, :], ps[:, 0:T], AF.Sigmoid), reads=[PB], writes=[sz.b[k]])
                ps, PB = linear(ring, "oc", oc, lambda kc: yA[:, kc, :], lambda kc: YAB[kc], T)
                ka = mm_.next()
                S.add("vector", tt(mt_[:, ka, :], szt[:, k, :], ps[:, 0:T], ALU.mult), reads=[sz.b[k], PB], writes=[mm_.b[ka]])
                ps, PB = linear(ring, "in4", 64 + oc, ur, ubf, T)
                k = sz.next()
                S.add("scalar", act(szt[:, k, :], ps[:, 0:T], AF.Sigmoid), reads=[PB], writes=[sz.b[k]])
                ps, PB = linear(ring, "od", oc, lambda kc: yB[:, kc, :], lambda kc: YBB[kc], T)
                kb = mm_.next()
                S.add("vector", tt(mt_[:, kb, :], szt[:, k, :], ps[:, 0:T], ALU.mult), reads=[sz.b[k], PB], writes=[mm_.b[kb]])
                S.add("vector", tt(mrg[:, oc, :], mt_[:, ka, :], mt_[:, kb, :], ALU.add), reads=[mm_.b[ka], mm_.b[kb]], writes=[MRB[oc]])
            for oc in range(32):
                ps, PB = linear(ring, "wo", oc, lambda kc: mrg[:, kc, :], lambda kc: MRB[kc], T)
                k = hc.next()
                S.add("sync", dma(hch[:, k, :], H1[:, oc, c0:c0 + T]), writes=[hc.b[k]], kind="d")
                S.add("vector", tt(hch[:, k, :], hch[:, k, :], ps[:, 0:T], ALU.add), reads=[hc.b[k], PB], writes=[hc.b[k]])
                S.add("scalar", dma(H1[:, oc, c0:c0 + T], hch[:, k, :]), reads=[hc.b[k]], writes=[Buf("t")], kind="d")
            S.emit(block)

    def stage4b(mt):
        c0 = mt * T
        SC = 128.0 ** -0.5
        with (sbt("b_hT", [128, 32, T], F32) as hT, sbt("b_xn", [128, 32, T], BF16) as xn,
              sbt("b_hid", [128, 22, T], BF16) as hid, sbt("b_ring", [128, 4, 32, 128], BF16) as ringt,
              sbt("b_sq", [128, 2, T], F32) as sqt, sbt("b_sg", [128, 2, T], F32) as sgt,
              sbt("b_rb", [128, T], F32) as rb, sbt("b_stg", [128, 2, T], F32) as stgt,
              sbt("b_qx", [128, 4, T], BF16) as qx, sbt("b_ox", [128, 4, T], BF16) as ox,
              sbt("b_pf", [128, 2, 256], F32) as pft, sbt("b_pn", [128, 2, 256], BF16) as pnt,
              sbt("b_pT", [128, 2, 2, 128], BF16) as pTt, sbt("b_st", [128, 2, 4], F32) as stt_,
              nc.Block() as block):
            HTB = [Buf("hT%d" % k) for k in range(32)]
            XNB = [Buf("xn%d" % k) for k in range(32)]
            HIDB = [Buf("hid%d" % k) for k in range(22)]
            QXB = [Buf("qx%d" % k) for k in range(4)]
            OXB = [Buf("ox%d" % k) for k in range(4)]
            RBB = Buf("rbb")
            ring = Ring(ringt, 4, "ringb")
            sq, sg, stg = Ring(sqt, 2, "sq"), Ring(sgt, 2, "sg"), Ring(stgt, 2, "stg")
            pf, pn, pT, st = Ring(pft, 2, "pf"), Ring(pnt, 2, "pn"), Ring(pTt, 2, "pT"), Ring(stt_, 2, "st")
            for kc in range(32):
                S.add("sync", dma(hT[:, kc, :], H1[:, kc, c0:c0 + T]), writes=[HTB[kc]], kind="d")
            hr, hb = (lambda kc: hT[:, kc, :]), (lambda kc: HTB[kc])
            xr, xb = (lambda kc: xn[:, kc, :]), (lambda kc: XNB[kc])
            rmsnorm(hr, hb, 2, xr, xb, T, sq, rb, RBB)
            for c in range(4):
                ps, PB = linear(ring, "xa", c, xr, xb, T)
                S.add("scalar", act(qx[:, c, :], ps[:, 0:T], AF.Copy), reads=[PB], writes=[QXB[c]])
            for hh in range(4):
                for tq in range(4):
                    tsl = slice(tq * 128, (tq + 1) * 128)
                    ps, PB = bank()
                    S.add("tensor", mm(ps[:, 0:256], qx[:, hh, tsl], KmT[:, hh, :]), reads=[QXB[hh], KMT], writes=[PB])
                    k = st.next()
                    S.add("vector", lambda e, ps=ps, k=k: e.tensor_reduce(out=stt_[:, k, 0:1], in_=ps[:, 0:256], axis=mybir.AxisListType.X, op=ALU.max),
                          reads=[PB], writes=[st.b[k]])
                    S.add("vector", ts(stt_[:, k, 1:2], stt_[:, k, 0:1], -SC, None, ALU.mult), reads=[st.b[k]], writes=[st.b[k]])
                    kf = pf.next()
                    S.add("scalar", act(pft[:, kf, :], ps[:, 0:256], AF.Exp, bias=stt_[:, k, 1:2], scale=SC, accum_out=stt_[:, k, 2:3]),
                          reads=[PB, st.b[k]], writes=[pf.b[kf], st.b[k]])
                    S.add("vector", recip(stt_[:, k, 3:4], stt_[:, k, 2:3]), reads=[st.b[k]], writes=[st.b[k]])
                    kn = pn.next()
                    S.add("vector", ts(pnt[:, kn, :], pft[:, kf, :], stt_[:, k, 3:4], None, ALU.mult), reads=[pf.b[kf], st.b[k]], writes=[pn.b[kn]])
                    kt = pT.next()
                    for mc in range(2):
                        ps2, PB2 = bank()
                        S.add("tensor", mm(ps2[:, 0:128], pnt[:, kn, mc * 128:(mc + 1) * 128], ibf[:, :]), reads=[pn.b[kn], IBF], writes=[PB2])
                        S.add("scalar", act(pTt[:, kt, mc, :], ps2[:, 0:128], AF.Copy), reads=[PB2], writes=[pT.b[kt]])
                    ps3, PB3 = bank()
                    S.add("tensor", mm(ps3[:, 0:128], Vm[:, 0, hh * 128:(hh + 1) * 128], pTt[:, kt, 0, :], True, False), reads=[VM, pT.b[kt]], writes=[PB3])
                    S.add("tensor", mm(ps3[:, 0:128], Vm[:, 1, hh * 128:(hh + 1) * 128], pTt[:, kt, 1, :], False, True), reads=[VM, pT.b[kt]], writes=[PB3])
                    S.add("vector", cp(ox[:, hh, tsl], ps3[:, 0:128]), reads=[PB3], writes=[OXB[hh]])
            for oc in range(32):
                ps, PB = linear(ring, "xo", oc, lambda kc: ox[:, kc, :], lambda kc: OXB[kc], T)
                S.add("vector", tt(hT[:, oc, :], hT[:, oc, :], ps[:, 0:T], ALU.add), reads=[PB, HTB[oc]], writes=[HTB[oc]])
            rmsnorm(hr, hb, 4, xr, xb, T, sq, rb, RBB)
            ffn(ring, "f2", xn, XNB, hT, HTB, hid, HIDB, sg)
            SB_ = [None]

            def fdst(kc):
                SB_[0] = stg.next()
                return stgt[:, SB_[0], :]

            ps, PB = bank()
            for kc in range(32):
                i = sq.next()
                S.add("scalar", act(sqt[:, i, :], hT[:, kc, :], AF.Square), reads=[HTB[kc]], writes=[sq.b[i]])
                S.add("tensor", mm(ps[:, 0:T], cst[:, C_ONE:C_ONE + 128], sqt[:, i, :], kc == 0, kc == 31),
                      reads=[sq.b[i], CST], writes=[PB] if kc in (0, 31) else [])
            S.add("scalar", act(rb[:, :], ps[:, 0:T], AF.Sqrt, bias=EPS, scale=1.0 / D), reads=[PB], writes=[RBB])
            S.add("vector", recip(rb[:, :], rb[:, :]), reads=[RBB], writes=[RBB])
            for kc in range(32):
                i = stg.next()
                S.add("vector", stt(stgt[:, i, :], hT[:, kc, :], gain(5, kc), rb[:, :], ALU.mult, ALU.mult),
                      reads=[HTB[kc], RBB, VEC], writes=[stg.b[i]])
                S.add("scalar", dma(yT_d[:, kc, c0:c0 + T], stgt[:, i, :]), reads=[stg.b[i]], writes=[Buf("t")], kind="d")
            S.emit(block)

    PAIRS = [list(range(8))]
    XI1 = nc.dram_tensor("XI1", [128, 176], F32, kind="Internal").ap()
    XO1 = nc.dram_tensor("XO1", [8 * 128, 176], F32, kind="Internal").ap()
    XI2 = nc.dram_tensor("XI2", [128, 2048], F32, kind="Internal").ap()
    XO2 = nc.dram_tensor("XO2", [8 * 128, 2048], F32, kind="Internal").ap()
    def oh(r):
        return vecs[:, V_OH + r:V_OH + r + 1]

    def exch1():
        tl = nmt * T
        with (sbt("x1a", [128, 176], F32) as hx, sbt("x1b", [128, 176], F32) as hx2,
              sbt("x1c", [128, 8, 176], F32) as hx8, nc.Block() as block):
            HX, HX2, BXI, BXO = Buf("hx"), Buf("hx2"), Buf("xi1"), Buf("xo1")
            for c in range(48):
                S.add("sync", dma(hx[:, 3 * c:3 * c + 3], QKV[c, :, tl:tl + 3]), writes=[HX], reads=[], kind="d")
                HX.writer = None
            for c in range(16):
                S.add("sync", dma(hx[:, 144 + 2 * c:146 + 2 * c], PCC[c, :, tl:tl + 2]), writes=[HX], reads=[], kind="d")
                HX.writer = None
            loads = S.ops["sync"][-64:]
            st_ = S.add("sync", dma(XI1, hx[:, :]), reads=[], writes=[BXI], kind="d")
            for d_ in loads:
                d_.signal = True
                st_.deps.append(d_)
            S.add("gpsimd", lambda e: e.collective_compute("AllGather", ALU.bypass, replica_groups=PAIRS, ins=[XI1], outs=[XO1]),
                  reads=[BXI], writes=[BXO], kind="cc")
            HX8 = Buf("hx8")
            S.add("sync", dma(hx8[:, :, :], XO1.rearrange("(r p) n -> p r n", p=128)), reads=[BXO], writes=[HX8], kind="d")
            S.add("vector", ts(hx2[:, :], hx8[:, 0, :], oh(0), None, ALU.mult), reads=[HX8, VEC], writes=[HX2])
            for r in range(1, 8):
                S.add("vector", stt(hx2[:, :], hx8[:, r, :], oh(r), hx2[:, :], ALU.mult, ALU.add), reads=[HX8, VEC, HX2], writes=[HX2])
            for c in range(48):
                S.add("sync", dma(QKV[c, :, 0:3], hx2[:, 3 * c:3 * c + 3]), reads=[HX2], writes=[Buf("t")], kind="d")
            for c in range(16):
                S.add("sync", dma(PCC[c, :, 0:2], hx2[:, 144 + 2 * c:146 + 2 * c]), reads=[HX2], writes=[Buf("t")], kind="d")
            S.emit(block)

    def exch2():
        with sbt("x2", [128, 8, 2048], F32) as sx, nc.Block() as block:
            SX, BXI, BXO = Buf("sx"), Buf("xi2"), Buf("xo2")
            S.add("sync", dma(XI2, Sst[:].rearrange("p a b -> p (a b)")), reads=[SST], writes=[BXI], kind="d")
            S.add("gpsimd", lambda e: e.collective_compute("AllGather", ALU.bypass, replica_groups=PAIRS, ins=[XI2], outs=[XO2]),
                  reads=[BXI], writes=[BXO], kind="cc")
            for r in range(8):
                S.add("sync", dma(sx[:, r, :], XO2[r * 128:(r + 1) * 128, :]), reads=[BXO], writes=[SX], kind="d")
                SX.writer = None
            lds = S.ops["sync"][-8:]
            sflat = Sst[:].rearrange("p a b -> p (a b)")
            op = S.add("vector", ts(sflat, sx[:, 0, :], oh(0), None, ALU.mult), reads=[VEC, SST], writes=[SST])
            for d_ in lds:
                d_.signal = True
                op.deps.append(d_)
            for r in range(1, 8):
                S.add("vector", stt(sflat, sx[:, r, :], oh(r), sflat, ALU.mult, ALU.add), reads=[VEC, SST], writes=[SST])
            S.emit(block)

    if "1" in stages:
        for mt in range(nmt):
            stage1(mt)
    if "2" in stages:
        if "x" in stages:
            exch1()
        for mt in range(nmt):
            stage2(mt)
        if "x" in stages:
            exch2()
            for mt in range(nmt):
                stage2(mt)
    if "M" in stages:
        stageM()
    for mt in range(nmt):
        if "4a" in stages:
            stage4a(mt)
        if "4b" in stages:
            stage4b(mt)
    if debug:
        with sbt("dbg", [128, 4096], F32) as dbg, nc.Block() as block:
            DB_ = Buf("dd")
            S.add("vector", cp(dbg[:, 0:1024], KmT[:].rearrange("p a b -> p (a b)")), writes=[DB_])
            S.add("vector", cp(dbg[:, 1024:2048], Vm[:].rearrange("p a b -> p (a b)")), writes=[DB_])
            S.add("vector", cp(dbg[:, 2048:4096], Sst[:, 0:16, :].rearrange("p a b -> p (a b)")), writes=[DB_])
            S.add("sync", dma(DBG, dbg[:]), reads=[DB_], writes=[Buf("d2")], kind="d")
            S.emit(block)
    return nc


_CACHE = {}


def kernel(**inputs):
    inp = {k: np.asarray(v) for k, v in inputs.items()}
    shards = _prep_weights(inp)
    vecs, consts = _prep_small(inp)
    x = inp["x"].astype(np.float32)
    mem = inp["mem"].astype(np.float32)
    in_maps = []
    for c in range(8):
        b, half = c // 2, c % 2
        m = dict(shards[c])
        xs = x[b, half * NTOK:(half + 1) * NTOK]
        m["xT"] = np.ascontiguousarray(xs.T.reshape(32, 128, NTOK).transpose(1, 0, 2))
        m["memT"] = np.ascontiguousarray(mem[b].T.reshape(32, 128, 256).transpose(1, 0, 2))
        vc = vecs.copy()
        if half == 1:
            vc[:, V_OH + c - 1] = 1.0
        m["vecs"] = vc
        m["consts"] = consts
        in_maps.append(m)
    if "nc" not in _CACHE:
        _CACHE["nc"] = build()
    res = run_bass_kernel_spmd(_CACHE["nc"], in_maps, core_ids=list(range(8)))
    out = np.empty((4, 2 * NTOK, D), np.float32)
    for c in range(8):
        b, half = c // 2, c % 2
        yT = np.asarray(res.results[c]["yT"])
        out[b, half * NTOK:(half + 1) * NTOK] = yT.transpose(2, 1, 0).reshape(NTOK, D)
    return out
```

```python
import numpy as np
import concourse.bass as bass
import concourse.mybir as mybir

F32 = mybir.dt.float32
BF16 = mybir.dt.bfloat16
AF = mybir.ActivationFunctionType
ALU = mybir.AluOpType

ENGS = ["tensor", "vector", "scalar", "gpsimd", "sync"]


class Buf:
    __slots__ = ("name", "writer", "readers", "excl")

    def __init__(self, name, excl=False):
        self.name = name
        self.writer = None
        self.readers = {}
        self.excl = excl


class Op:
    __slots__ = ("eng", "fn", "deps", "signal", "sem", "val", "kind", "epoch")

    def __init__(self, eng, fn, kind):
        self.eng = eng
        self.fn = fn
        self.kind = kind
        self.deps = []
        self.signal = False
        self.sem = None
        self.val = 0
        self.epoch = 0


class Sched:
    def __init__(self, nc, ndma=12):
        self.nc = nc
        self.ops = {e: [] for e in ENGS}
        self.sems = {e: nc.semaphore("sem_" + e).__enter__() for e in ENGS}
        self.cnt = {e: 0 for e in ENGS}
        self.dsems = {e: [nc.semaphore("dsem_%s_%d" % (e, i)).__enter__() for i in range(ndma)]
                      for e in ("gpsimd", "sync", "scalar")}
        self.dcnt = {e: [0] * ndma for e in ("gpsimd", "sync", "scalar")}
        self.drr = {"gpsimd": 0, "sync": 0, "scalar": 0}
        self.waited = {e: {} for e in ENGS}
        self.pending = []
        self.alldma = []
        self.last = {e: None for e in ENGS}
        self.ccs = []
        self.epoch = 0
        self.nsem = 0

    def add(self, eng, fn, reads=(), writes=(), kind="c"):
        op = Op(eng, fn, kind)
        op.epoch = self.epoch
        deps = []

        def dep(o):
            if o is None or o is op or (o.epoch < self.epoch and o.kind != "cc"):
                return
            if o.eng == eng and o.kind == "c" and kind == "c" and eng == "tensor":
                return
            deps.append(o)

        for b in reads:
            if b.excl:
                if b.writer is not None and b.writer.eng != eng:
                    dep(b.writer)
                for e2, o in b.readers.items():
                    if e2 != eng:
                        dep(o)
            else:
                dep(b.writer)
        for b in writes:
            if b.excl:
                if b.writer is not None and b.writer.eng != eng:
                    dep(b.writer)
                for e2, o in b.readers.items():
                    if e2 != eng:
                        dep(o)
            else:
                dep(b.writer)
                for o in b.readers.values():
                    dep(o)
        for b in reads:
            if kind in ("d", "cc"):
                b.readers[("dma", id(op))] = op
            else:
                b.readers[eng] = op
        for b in writes:
            b.writer = op
            b.readers = {}
        seen = set()
        for d in deps:
            if id(d) not in seen:
                seen.add(id(d))
                d.signal = True
                op.deps.append(d)
        self.ops[eng].append(op)
        self.pending.append(op)
        if kind in ("d", "cc"):
            self.alldma.append(op)
        self.last[eng] = op
        return op

    def barrier(self):
        lasts = [o for o in self.last.values() if o is not None and o.kind == "c"]
        dmas = [d for d in self.alldma if d.kind != "cc"]
        self.alldma = []
        for e in ENGS:
            op = Op(e, None, "w")
            op.epoch = self.epoch
            for d in lasts + dmas:
                if d.eng == e and d.kind == "c":
                    continue
                d.signal = True
                op.deps.append(d)
            self.ops[e].append(op)
            self.pending.append(op)

    def emit(self, block):
        nc = self.nc
        self.barrier()
        self.epoch += 1
        pend = self.pending
        self.pending = []
        for e in ENGS:
            if self.cnt[e] > 1500:
                self.nsem += 1
                self.sems[e] = nc.semaphore("sem_%s_%d" % (e, self.nsem)).__enter__()
                self.cnt[e] = 0
        for op in pend:
            if op.kind == "c":
                if op.signal:
                    self.cnt[op.eng] += 1
                    op.sem = self.sems[op.eng]
                    op.val = self.cnt[op.eng]
            elif op.kind == "d":
                i = self.drr[op.eng]
                self.drr[op.eng] = (i + 1) % len(self.dsems[op.eng])
                self.dcnt[op.eng][i] += 16
                op.sem = self.dsems[op.eng][i]
                op.val = self.dcnt[op.eng][i]
            elif op.kind == "cc":
                s = nc.semaphore("ccsem%d" % len(self.ccs)).__enter__()
                self.ccs.append(s)
                op.sem = s
                op.val = 1
        per = {e: [o for o in pend if o.eng == e] for e in ENGS}

        def make(e):
            def body(eng):
                w = self.waited[e]
                for op in per[e]:
                    for d in op.deps:
                        k = id(d.sem)
                        if w.get(k, 0) < d.val:
                            eng.wait_ge(d.sem, d.val)
                            w[k] = d.val
                    if op.fn is None:
                        continue
                    ins = op.fn(eng)
                    if op.kind == "d":
                        ins.then_inc(op.sem, 16)
                    elif op.kind == "cc":
                        ins.then_inc(op.sem, 1)
                    elif op.signal:
                        ins.then_inc(op.sem, 1)
            return body

        for e in ENGS:
            if per[e]:
                getattr(block, e)(make(e))

from concourse.bass_utils import run_bass_kernel_spmd

D = 4096
FF = 11008
NTOK = 1024
T = 512
NMT = NTOK // T
NH = 16
EPS = 1e-6
HG = [(0, 22), (22, 44), (44, 65), (65, 86)]
NSLOT = 6
PSZ = 4

GROUPS = {
    "f1gu": (176, 32), "f1d": (128, 22), "in1": (88, 32), "in4": (96, 32), "oc": (32, 16),
    "od": (32, 16), "wo": (32, 32), "xa": (16, 32), "xo": (32, 4), "f2gu": (176, 32), "f2d": (128, 22),
}
GORDER = ["f1gu", "f1d", "in1", "in4", "oc", "od", "wo", "xa", "xo", "f2gu", "f2d"]

V_GAIN = 0
V_CA = 192
V_CQ = 240
V_ALOG = 432
V_DTB = 448
V_DNG = 464
V_OH = 468
NV = 476
C_I, C_ONE, C_NEG, C_TRI, C_UPP, C_NMS, C_NMT = 0, 128, 256, 384, 448, 512, 576
NCC = 640


def _mkchunk(Wsub, kc):
    K, n = Wsub.shape
    out = np.zeros((128, kc, 128), np.float32)
    k = K // 128
    out[:, :k, :n] = Wsub.reshape(k, 128, n).transpose(1, 0, 2)
    return out


def _prep_weights(inp):
    g = {}
    def ffn(pre, wg, wu, wd):
        gu = []
        for j in range(86):
            gu.append(_mkchunk(wg[:, j * 128:(j + 1) * 128], 32))
            gu.append(_mkchunk(wu[:, j * 128:(j + 1) * 128], 32))
        g[pre + "gu"] = gu
        dd = []
        for (a, b) in HG:
            for oc in range(32):
                dd.append(_mkchunk(wd[a * 128:b * 128, oc * 128:(oc + 1) * 128], 22))
        g[pre + "d"] = dd
    ffn("f1", inp["ffn1_w_gate"][0], inp["ffn1_w_up"][0], inp["ffn1_w_down"][0])
    ffn("f2", inp["ffn2_w_gate"][0], inp["ffn2_w_up"][0], inp["ffn2_w_down"][0])
    wi = inp["w_in"][0]
    c1 = []
    for base in (6144, 8192, 10240):
        for i in range(16):
            c1.append(_mkchunk(wi[:, base + 128 * i: base + 128 * (i + 1)], 32))
    c1.append(_mkchunk(wi[:, 14336:14368], 32))
    for i in range(16):
        c1.append(_mkchunk(wi[:, 128 * i:128 * (i + 1)], 32))
        c1.append(_mkchunk(wi[:, 2048 + 128 * i:2048 + 128 * (i + 1)], 32))
    g["in1"] = c1
    c4 = []
    for base, n in ((4096, 16), (12288, 16), (14368, 32), (18464, 32)):
        for i in range(n):
            c4.append(_mkchunk(wi[:, base + 128 * i: base + 128 * (i + 1)], 32))
    g["in4"] = c4
    g["oc"] = [_mkchunk(inp["w_out_conv"][0][:, 128 * i:128 * (i + 1)], 16) for i in range(32)]
    g["od"] = [_mkchunk(inp["w_out_delta"][0][:, 128 * i:128 * (i + 1)], 16) for i in range(32)]
    g["wo"] = [_mkchunk(inp["w_o"][0][:, 128 * i:128 * (i + 1)], 32) for i in range(32)]
    g["xa"] = [_mkchunk(inp[k][0][:, 128 * i:128 * (i + 1)], 32)
               for k in ("xattn_wq", "xattn_wk", "xattn_wv") for i in range(4)]
    g["xo"] = [_mkchunk(inp["xattn_wo"][0][:, 128 * i:128 * (i + 1)], 4) for i in range(32)]
    shards = [dict() for _ in range(8)]
    for name, (n, kc) in GROUPS.items():
        ch = g[name]
        while len(ch) < n:
            ch.append(np.zeros((128, kc, 128), np.float32))
        for r in range(8):
            arr = np.stack([ch[l * 8 + r] for l in range(n // 8)], 0)
            shards[r]["w_" + name] = np.ascontiguousarray(arr.reshape(n // 8 * 128, kc * 128))
    return shards


def _prep_small(inp):
    vecs = np.zeros((128, NV), np.float32)
    for gi, k in enumerate(["ffn1_norm", "mix_norm", "xattn_norm", "mem_norm", "ffn2_norm", "final_norm"]):
        v = np.asarray(inp[k]).reshape(-1)
        vecs[:, V_GAIN + gi * 32:V_GAIN + (gi + 1) * 32] = v.reshape(32, 128).T
    cw = inp["conv_w"][0]
    vecs[:, V_CA:V_CA + 48] = cw.reshape(3, 16, 128).transpose(2, 1, 0).reshape(128, 48)
    qw = inp["qkv_conv_w"][0]
    vecs[:, V_CQ:V_CQ + 192] = qw.reshape(4, 48, 128).transpose(2, 1, 0).reshape(128, 192)
    vecs[:, V_ALOG:V_ALOG + 16] = inp["a_log"][0][None, :]
    vecs[:, V_DTB:V_DTB + 16] = inp["dt_bias"][0][None, :]
    vecs[:, V_DNG] = inp["dn_out_norm"][0]
    c = np.zeros((128, NCC), np.float32)
    c[:, C_I:C_I + 128] = np.eye(128)
    c[:, C_ONE:C_ONE + 128] = 1.0
    c[:, C_NEG:C_NEG + 128] = -1.0
    t = np.arange(64)
    c[:64, C_TRI:C_TRI + 64] = (t[:, None] <= t[None, :])
    c[:64, C_UPP:C_UPP + 64] = (t[:, None] > t[None, :])
    c[:64, C_NMS:C_NMS + 64] = np.where(t[:, None] > t[None, :], 0.0, -30000.0)
    c[:64, C_NMT:C_NMT + 64] = np.where(t[None, :] >= t[:, None], 0.0, -30000.0)
    return vecs, c


def build(debug=False, stages=("M", "1", "2", "x", "4a", "4b"), nmt=NMT, groups=None, ntok=NTOK):
    groups = list(GORDER) if groups is None else groups
    nc = bass.Bass("TRN2", target_bir_lowering=False)
    dk = "ExternalOutput" if debug else "Internal"
    xT_d = nc.dram_tensor("xT", [128, 32, ntok], F32, kind="ExternalInput").ap()
    memT_d = nc.dram_tensor("memT", [128, 32, 256], F32, kind="ExternalInput").ap()
    vecs_d = nc.dram_tensor("vecs", [128, NV], F32, kind="ExternalInput").ap()
    consts_d = nc.dram_tensor("consts", [128, NCC], F32, kind="ExternalInput").ap()
    yT_d = nc.dram_tensor("yT", [128, 32, ntok], F32, kind="ExternalOutput").ap()
    w_ext, w_cc, w_all = {}, {}, {}
    for name in groups:
        n, kc = GROUPS[name]
        w_ext[name] = nc.dram_tensor("w_" + name, [n // 8 * 128, kc * 128], F32, kind="ExternalInput").ap()
        w_cc[name] = nc.dram_tensor("cc_" + name, [n // 8 * 128, kc * 128], BF16, kind="Internal").ap()
        w_all[name] = []
        for p0 in range(0, n // 8, PSZ):
            psz = min(PSZ, n // 8 - p0)
            w_all[name].append(nc.dram_tensor("wall_%s_%d" % (name, p0), [8 * psz * 128, kc * 128], BF16, kind="Internal").ap())
    H1 = nc.dram_tensor("H1", [128, 32, ntok], F32, kind=dk).ap()
    U1 = nc.dram_tensor("U1", [128, 32, ntok], BF16, kind="Internal").ap()
    DBG = nc.dram_tensor("DBG", [128, 4096], F32, kind=dk).ap()
    QKV = nc.dram_tensor("QKV", [48, 128, 3 + ntok], F32, kind=dk).ap()
    BAp = nc.dram_tensor("BAp", [32, ntok], F32, kind=dk).ap()
    PCC = nc.dram_tensor("PCC", [16, 128, 2 + ntok], F32, kind=dk).ap()
    YB = nc.dram_tensor("YB", [16, 128, ntok], F32, kind=dk).ap()

    S = Sched(nc)
    ctx = []

    uniq = [0]

    def sbt(name, shape, dt):
        uniq[0] += 1
        return nc.sbuf_tensor("%s_%d" % (name, uniq[0]), shape, dt)

    pers = [
        nc.sbuf_tensor("vecs_s", [128, NV], F32), nc.sbuf_tensor("consts_s", [128, NCC], F32),
        nc.sbuf_tensor("ibf", [128, 128], BF16), nc.sbuf_tensor("Sst", [128, NH, 128], F32),
        nc.sbuf_tensor("negA", [128, 16], F32), nc.sbuf_tensor("KmT", [128, 4, 256], BF16),
        nc.sbuf_tensor("Vm", [128, 2, 512], BF16), nc.sbuf_tensor("zero", [128, 64], F32),
    ]
    vecs, cst, ibf, Sst, negA, KmT, Vm, zero = [p.__enter__() for p in pers]
    psum = [nc.psum_tensor("ps%d" % i, [128, 512], F32).__enter__() for i in range(8)]
    PSB = [Buf("ps%d" % i, True) for i in range(8)]
    pstate = [0]

    def bank():
        i = pstate[0]
        pstate[0] = (i + 1) % 8
        return psum[i], PSB[i]

    VEC, CST, IBF, SST, NEGA, KMT, VM, ZERO = [Buf(n) for n in ("vecs", "cst", "ibf", "sst", "nega", "kmt", "vm", "zero")]
    WALL = {g: [Buf("wall_%s_%d" % (g, p)) for p in range((GROUPS[g][0] // 8 + PSZ - 1) // PSZ)] for g in GROUPS}
    if debug:
        for _n in ("U1",):
            pass
    B_H1 = [Buf("H1_%d" % m) for m in range(NMT)]
    B_U1 = [Buf("U1_%d" % m) for m in range(NMT)]
    B_QKV, B_BA, B_PCC, B_YB, B_OUT = Buf("qkv"), Buf("ba"), Buf("pcc"), Buf("yb"), Buf("out")

    def I128():
        return cst[:, C_I:C_I + 128]

    def gain(gi, kc):
        return vecs[:, V_GAIN + gi * 32 + kc:V_GAIN + gi * 32 + kc + 1]

    def mm(o, l, r, st=True, sp=True):
        return lambda e: e.matmul(o, l, r, start=st, stop=sp)

    def act(o, i, f, **kw):
        return lambda e: e.activation(o, i, f, **kw)

    def dma(o, i):
        return lambda e: e.dma_start(out=o, in_=i)

    def tt(o, a, b, op):
        return lambda e: e.tensor_tensor(o, a, b, op)

    def stt(o, a, s, b, op0, op1):
        return lambda e: e.scalar_tensor_tensor(out=o, in0=a, scalar=s, in1=b, op0=op0, op1=op1)

    def ts(o, a, s1, s2, op0, op1=None):
        if op1 is None:
            return lambda e: e.tensor_scalar(o, a, s1, None, op0)
        return lambda e: e.tensor_scalar(o, a, s1, s2, op0, op1)

    def cp(o, i):
        return lambda e: e.tensor_copy(o, i)

    def recip(o, i):
        return lambda e: e.reciprocal(o, i)

    with nc.Block() as block:
        S.add("sync", dma(vecs[:], vecs_d), writes=[VEC], kind="d")
        S.add("sync", dma(cst[:], consts_d), writes=[CST], kind="d")
        S.add("vector", cp(ibf[:], cst[:, C_I:C_I + 128]), reads=[CST], writes=[IBF])
        S.add("vector", lambda e: e.memset(Sst[:], 0.0), writes=[SST])
        S.add("vector", lambda e: e.memset(zero[:], 0.0), writes=[ZERO])
        S.add("scalar", act(negA[:], vecs[:, V_ALOG:V_ALOG + 16], AF.Exp), reads=[VEC], writes=[NEGA])
        S.add("vector", ts(negA[:], negA[:], -1.0, None, ALU.mult), reads=[NEGA], writes=[NEGA])
        for c in range(48):
            S.add("sync", dma(QKV[c, :, 0:3], zero[:, 0:3]), reads=[ZERO], writes=[B_QKV], kind="d")
        for c in range(16):
            S.add("sync", dma(PCC[c, :, 0:2], zero[:, 0:2]), reads=[ZERO], writes=[B_PCC], kind="d")
        for g in groups:
            n, kc = GROUPS[g]
            for pi, p0 in enumerate(range(0, n // 8, PSZ)):
                psz = min(PSZ, n // 8 - p0)
                CCB = Buf("cc_%s_%d" % (g, pi))
                for l in range(p0, p0 + psz):
                    S.add("gpsimd", dma(w_cc[g][l * 128:(l + 1) * 128, :], w_ext[g][l * 128:(l + 1) * 128, :]), writes=[Buf("t")], reads=[], kind="d")
                    CCB.readers = {}
                lastd = S.ops["gpsimd"][-psz:]
                op = S.add("gpsimd", (lambda g=g, pi=pi, p0=p0, psz=psz: lambda e: e.collective_compute(
                    "AllGather", ALU.bypass, replica_groups=[list(range(8))],
                    ins=[w_cc[g][p0 * 128:(p0 + psz) * 128, :]], outs=[w_all[g][pi]]))(),
                    reads=[], writes=[WALL[g][pi]], kind="cc")
                for d_ in lastd:
                    d_.signal = True
                    op.deps.append(d_)
        S.emit(block)

    class Ring:
        def __init__(self, t, n, name):
            self.t, self.n, self.i = t, n, 0
            self.b = [Buf("%s%d" % (name, k)) for k in range(n)]

        def next(self):
            i = self.i
            self.i = (i + 1) % self.n
            return i

    def load_w(ring, g, idx):
        n, kc = GROUPS[g]
        l = idx // 8
        pi = l // PSZ
        psz = min(PSZ, n // 8 - pi * PSZ)
        row = (idx % 8) * psz + (l - pi * PSZ)
        s = ring.next()
        S.add("sync", dma(ring.t[:, s, 0:kc, :], w_all[g][pi][row * 128:(row + 1) * 128, :].rearrange("p (k n) -> p k n", n=128)),
              reads=[WALL[g][pi]], writes=[ring.b[s]], kind="d")
        return s

    def linear(ring, g, idx, rhs, rhsb, W, M=128, nk=None):
        kc_n = GROUPS[g][1] if nk is None else nk
        s = load_w(ring, g, idx)
        ps, PB = bank()
        for kc in range(kc_n):
            S.add("tensor", mm(ps[0:M, 0:W], ring.t[:, s, kc, 0:M], rhs(kc), kc == 0, kc == kc_n - 1),
                  reads=[ring.b[s], rhsb(kc)], writes=[PB] if kc in (0, kc_n - 1) else [])
        return ps, PB

    def rmsnorm(src, srcb, gi, dst, dstb, W, sq, rb, RBB, nk=32):
        ps, PB = bank()
        for kc in range(nk):
            i = sq.next()
            S.add("scalar", act(sq.t[:, i, 0:W], src(kc), AF.Square), reads=[srcb(kc)], writes=[sq.b[i]])
            S.add("tensor", mm(ps[:, 0:W], cst[:, C_ONE:C_ONE + 128], sq.t[:, i, 0:W], kc == 0, kc == nk - 1),
                  reads=[sq.b[i], CST], writes=[PB] if kc in (0, nk - 1) else [])
        S.add("scalar", act(rb[:, 0:W], ps[:, 0:W], AF.Sqrt, bias=EPS, scale=1.0 / D), reads=[PB], writes=[RBB])
        S.add("vector", recip(rb[:, 0:W], rb[:, 0:W]), reads=[RBB], writes=[RBB])
        for kc in range(nk):
            S.add("vector", stt(dst(kc), src(kc), gain(gi, kc), rb[:, 0:W], ALU.mult, ALU.mult),
                  reads=[srcb(kc), RBB, VEC], writes=[dstb(kc)])

    def ffn(ring, pre, xn, XNB, hT, HTB, hid, HIDB, sg):
        for gi_, (a, b) in enumerate(HG):
            for j in range(a, b):
                pg, PG = linear(ring, pre + "gu", 2 * j, lambda kc: xn[:, kc, :], lambda kc: XNB[kc], T)
                pu, PU = linear(ring, pre + "gu", 2 * j + 1, lambda kc: xn[:, kc, :], lambda kc: XNB[kc], T)
                i = sg.next()
                S.add("scalar", act(sg.t[:, i, :], pg[:, 0:T], AF.Silu), reads=[PG], writes=[sg.b[i]])
                S.add("vector", tt(hid[:, j - a, :], sg.t[:, i, :], pu[:, 0:T], ALU.mult),
                      reads=[sg.b[i], PU], writes=[HIDB[j - a]])
            for oc in range(32):
                pd, PD = linear(ring, pre + "d", gi_ * 32 + oc, lambda kc: hid[:, kc, :], lambda kc: HIDB[kc], T, nk=b - a)
                S.add("vector", stt(hT[:, oc, :], pd[:, 0:T], 0.5, hT[:, oc, :], ALU.mult, ALU.add),
                      reads=[PD, HTB[oc]], writes=[HTB[oc]])

    def stage1(mt):
        c0 = mt * T
        with (sbt("s1_hT", [128, 32, T], F32) as hT, sbt("s1_xn", [128, 32, T], BF16) as xn,
              sbt("s1_hid", [128, 22, T], BF16) as hid, sbt("s1_ring", [128, NSLOT, 32, 128], BF16) as ringt,
              sbt("s1_sq", [128, 2, T], F32) as sqt, sbt("s1_sg", [128, 2, T], F32) as sgt,
              sbt("s1_rb", [128, T], F32) as rb, sbt("s1_stg", [128, 4, T], F32) as stgt,
              nc.Block() as block):
            HTB = [Buf("hT%d" % k) for k in range(32)]
            XNB = [Buf("xn%d" % k) for k in range(32)]
            HIDB = [Buf("hid%d" % k) for k in range(22)]
            RBB = Buf("rb")
            ring = Ring(ringt, NSLOT, "ring")
            sq, sg, stg = Ring(sqt, 2, "sq"), Ring(sgt, 2, "sg"), Ring(stgt, 4, "stg")
            for kc in range(32):
                S.add("sync", dma(hT[:, kc, :], xT_d[:, kc, c0:c0 + T]), writes=[HTB[kc]], kind="d")
            rmsnorm(lambda kc: hT[:, kc, :], lambda kc: HTB[kc], 0, lambda kc: xn[:, kc, :], lambda kc: XNB[kc], T, sq, rb, RBB)
            import os
            lvl = int(os.environ.get("DBG_S1", "9"))
            if lvl >= 2:
                ffn(ring, "f1", xn, XNB, hT, HTB, hid, HIDB, sg)
            rmsnorm(lambda kc: hT[:, kc, :], lambda kc: HTB[kc], 1, lambda kc: xn[:, kc, :], lambda kc: XNB[kc], T, sq, rb, RBB)
            for kc in range(32):
                S.add("scalar", dma(H1[:, kc, c0:c0 + T], hT[:, kc, :]), reads=[HTB[kc]], writes=[Buf("t")], kind="d")
                S.add("scalar", dma(U1[:, kc, c0:c0 + T], xn[:, kc, :]), reads=[XNB[kc]], writes=[Buf("t")], kind="d")
            xr, xb = (lambda kc: xn[:, kc, :]), (lambda kc: XNB[kc])
            if lvl < 3:
                S.emit(block)
                return
            for c in range(48):
                ps, PB = linear(ring, "in1", c, xr, xb, T)
                i = stg.next()
                S.add("scalar", act(stg.t[:, i, :], ps[:, 0:T], AF.Copy), reads=[PB], writes=[stg.b[i]])
                S.add("scalar", dma(QKV[c, :, 3 + c0:3 + c0 + T], stg.t[:, i, :]), reads=[stg.b[i]], writes=[Buf("t")], kind="d")
            ps, PB = linear(ring, "in1", 48, xr, xb, T, M=32)
            i = stg.next()
            S.add("scalar", act(stg.t[0:32, i, :], ps[0:32, 0:T], AF.Copy), reads=[PB], writes=[stg.b[i]])
            S.add("scalar", dma(BAp[:, c0:c0 + T], stg.t[0:32, i, :]), reads=[stg.b[i]], writes=[Buf("t")], kind="d")
            for k in range(16):
                ps, PB = linear(ring, "in1", 49 + 2 * k, xr, xb, T)
                i = stg.next()
                S.add("scalar", act(stg.t[:, i, :], ps[:, 0:T], AF.Copy), reads=[PB], writes=[stg.b[i]])
                ps2, PB2 = linear(ring, "in1", 50 + 2 * k, xr, xb, T)
                i2 = stg.next()
                S.add("vector", tt(stg.t[:, i2, :], stg.t[:, i, :], ps2[:, 0:T], ALU.mult), reads=[stg.b[i], PB2], writes=[stg.b[i2]])
                S.add("scalar", dma(PCC[k, :, 2 + c0:2 + c0 + T], stg.t[:, i2, :]), reads=[stg.b[i2]], writes=[Buf("t")], kind="d")
            S.emit(block)

    def stage2(mt, full=True):
        c0 = mt * T
        with (sbt("s2_baT", [32, T], F32) as baT, sbt("s2_tm", [128, 7, 8, 16], F32) as tm,
              sbt("s2_pre", [128, 4, 3, 3 + T], F32) as pre, sbt("s2_qk", [128, 4, 3, T], F32) as qk,
              sbt("s2_cv", [128, T], F32) as cv, sbt("s2_sqb", [128, T], F32) as sqb,
              sbt("s2_rb", [128, T], F32) as rb, sbt("s2_yb", [128, 4, T], F32) as yb,
              sbt("s2_t64", [64, 4, 10, 64], F32) as t64, sbt("s2_t64w", [64, 4, 6, 128], F32) as t64w,
              sbt("s2_t128", [128, 4, 3, 64], F32) as t128, sbt("s2_ssq", [64, 4, 2], F32) as ssq,
              nc.Block() as block):
            BAT, TM, CV, SQB, RBB = Buf("bat"), Buf("tm"), Buf("cv"), Buf("sqb"), Buf("rb2")
            PREB = [Buf("pre%d" % i) for i in range(4)]
            HB = [Buf("hb%d" % i) for i in range(4)]
            YBB = [Buf("yb%d" % i) for i in range(4)]
            N64 = ["Gtri", "decs", "decT", "A0", "A1", "B0", "B1", "X0", "X1", "AQT"]
            N64W = ["KBG", "KT", "VB", "UB", "U", "ON"]
            N128 = ["egcb", "QDT", "WDT"]
            UB_ = [{n: Buf(n + str(s)) for n in N64 + N64W + N128 + ["ssq"]} for s in range(4)]
            G_, BETA, EGC, EKT, BEG, NB, EGL = range(7)

            def tmv(k, c, p=64):
                return tm[0:p, k, c, :]

            ONE64 = cst[0:64, C_ONE:C_ONE + 64]
            NEG64 = cst[0:64, C_NEG:C_NEG + 64]
            I64 = cst[0:64, C_I:C_I + 64]
            S.add("sync", dma(baT[:, :], BAp[:, c0:c0 + T]), writes=[BAT], kind="d")
            for c in range(8):
                ps, PB = bank()
                S.add("tensor", mm(ps[0:64, 0:32], baT[0:32, c * 64:(c + 1) * 64], cst[0:32, C_I:C_I + 32]), reads=[BAT, CST], writes=[PB])
                S.add("scalar", act(tmv(BETA, c), ps[0:64, 0:16], AF.Sigmoid), reads=[PB], writes=[TM])
                S.add("vector", tt(tmv(G_, c), ps[0:64, 16:32], vecs[0:64, V_DTB:V_DTB + 16], ALU.add), reads=[PB, VEC, TM], writes=[TM])
                S.add("scalar", act(tmv(G_, c), tmv(G_, c), AF.Exp), reads=[TM], writes=[TM])
                S.add("scalar", act(tmv(G_, c), tmv(G_, c), AF.Ln, bias=1.0), reads=[TM], writes=[TM])
                S.add("vector", tt(tmv(G_, c), tmv(G_, c), negA[0:64, :], ALU.mult), reads=[TM, NEGA], writes=[TM])
                ps2, PB2 = bank()
                S.add("tensor", mm(ps2[0:64, 0:16], cst[0:64, C_TRI:C_TRI + 64], tmv(G_, c)), reads=[TM, CST], writes=[PB2])
                S.add("tensor", mm(ps2[0:64, 16:32], cst[0:64, C_UPP:C_UPP + 64], tmv(G_, c)), reads=[TM, CST])
                S.add("tensor", mm(ps2[:, 32:48], cst[0:64, C_ONE:C_ONE + 128], tmv(G_, c)), reads=[TM, CST], writes=[PB2])
                S.add("scalar", act(tmv(EGC, c), ps2[0:64, 0:16], AF.Exp), reads=[PB2, TM], writes=[TM])
                S.add("scalar", act(tmv(EKT, c), ps2[0:64, 16:32], AF.Exp), reads=[PB2, TM], writes=[TM])
                S.add("scalar", act(tmv(EGL, c, 128), ps2[:, 32:48], AF.Exp), reads=[PB2, TM], writes=[TM])
                S.add("vector", tt(tmv(BEG, c), tmv(BETA, c), tmv(EGC, c), ALU.mult), reads=[TM], writes=[TM])
                S.add("vector", ts(tmv(NB, c), tmv(BETA, c), -1.0, None, ALU.mult), reads=[TM], writes=[TM])

            def prep(h, hs):
                for qi in range(3):
                    cidx = qi * 16 + h
                    S.add("sync", dma(pre[:, hs, qi, :], QKV[cidx, :, c0:c0 + 3 + T]), writes=[PREB[hs]], kind="d")
                for qi in range(3):
                    cidx = qi * 16 + h
                    wv = lambda i, cidx=cidx: vecs[:, V_CQ + cidx * 4 + i:V_CQ + cidx * 4 + i + 1]
                    S.add("vector", ts(cv[:, :], pre[:, hs, qi, 0:T], wv(0), None, ALU.mult), reads=[PREB[hs], VEC], writes=[CV])
                    for i in range(1, 4):
                        S.add("vector", stt(cv[:, :], pre[:, hs, qi, i:i + T], wv(i), cv[:, :], ALU.mult, ALU.add),
                              reads=[PREB[hs], VEC, CV], writes=[CV])
                    S.add("scalar", act(qk[:, hs, qi, :], cv[:, :], AF.Silu), reads=[CV], writes=[HB[hs]])
                for qi in (0, 1):
                    S.add("scalar", act(sqb[:, :], qk[:, hs, qi, :], AF.Square), reads=[HB[hs]], writes=[SQB])
                    ps, PB = bank()
                    S.add("tensor", mm(ps[:, 0:T], cst[:, C_ONE:C_ONE + 128], sqb[:, :]), reads=[SQB, CST], writes=[PB])
                    S.add("scalar", act(rb[:, :], ps[:, 0:T], AF.Sqrt, bias=EPS, scale=1.0), reads=[PB], writes=[RBB])
                    S.add("vector", recip(rb[:, :], rb[:, :]), reads=[RBB], writes=[RBB])
                    S.add("vector", stt(qk[:, hs, qi, :], qk[:, hs, qi, :], (128.0 ** -0.5) if qi == 0 else 1.0, rb[:, :], ALU.mult, ALU.mult),
                          reads=[HB[hs], RBB], writes=[HB[hs]])
            SH = [Buf("sst%d" % i) for i in range(NH)]

            def unit(h, c, hs, us):
                if True:
                    ub = UB_[us]
                    ub = UB_[us]
                    cs = slice(c * 64, (c + 1) * 64)
                    qT, kT, vT = qk[:, hs, 0, cs], qk[:, hs, 1, cs], qk[:, hs, 2, cs]
                    a64 = lambda n: t64[:, us, N64.index(n), :]
                    a64w = lambda n: t64w[:, us, N64W.index(n), :]
                    a128 = lambda n: t128[:, us, N128.index(n), :]
                    sc = lambda k: tm[0:64, k, c, h:h + 1]
                    S.add("vector", ts(a64("Gtri"), cst[0:64, C_TRI:C_TRI + 64], sc(G_), None, ALU.mult), reads=[CST, TM], writes=[ub["Gtri"]])
                    yield
                    ps, PB = bank()
                    S.add("tensor", mm(ps[0:64, 0:64], a64("Gtri"), ONE64, True, False), reads=[ub["Gtri"], CST], writes=[PB])
                    S.add("tensor", mm(ps[0:64, 0:64], NEG64, a64("Gtri"), False, False), reads=[ub["Gtri"], CST])
                    S.add("tensor", mm(ps[0:64, 0:64], I64, cst[0:64, C_NMS:C_NMS + 64], False, True), reads=[CST], writes=[PB])
                    S.add("scalar", act(a64("decs"), ps[0:64, 0:64], AF.Exp), reads=[PB], writes=[ub["decs"]])
                    if full:
                        yield
                        ps, PB = bank()
                        S.add("tensor", mm(ps[0:64, 0:64], ONE64, a64("Gtri"), True, False), reads=[ub["Gtri"], CST], writes=[PB])
                        S.add("tensor", mm(ps[0:64, 0:64], a64("Gtri"), NEG64, False, False), reads=[ub["Gtri"], CST])
                        S.add("tensor", mm(ps[0:64, 0:64], I64, cst[0:64, C_NMT:C_NMT + 64], False, True), reads=[CST], writes=[PB])
                        S.add("scalar", act(a64("decT"), ps[0:64, 0:64], AF.Exp), reads=[PB], writes=[ub["decT"]])
                        yield
                        ps, PB = bank()
                        S.add("tensor", mm(ps[:, 0:64], cst[0:64, C_ONE:C_ONE + 128], a64("Gtri")), reads=[ub["Gtri"], CST], writes=[PB])
                        S.add("scalar", act(a128("egcb"), ps[:, 0:64], AF.Exp), reads=[PB], writes=[ub["egcb"]])
                        S.add("vector", tt(a128("QDT"), qT, a128("egcb"), ALU.mult), reads=[HB[hs], ub["egcb"]], writes=[ub["QDT"]])
                    yield
                    ps, PB = bank()
                    S.add("tensor", mm(ps[0:64, 0:64], kT, kT), reads=[HB[hs]], writes=[PB])
                    S.add("vector", stt(a64("A0"), ps[0:64, 0:64], sc(NB), a64("decs"), ALU.mult, ALU.mult), reads=[PB, TM, ub["decs"]], writes=[ub["A0"]])
                    if full:
                        yield
                        ps, PB = bank()
                        S.add("tensor", mm(ps[0:64, 0:64], kT, qT), reads=[HB[hs]], writes=[PB])
                        S.add("vector", tt(a64("AQT"), ps[0:64, 0:64], a64("decT"), ALU.mult), reads=[PB, ub["decT"]], writes=[ub["AQT"]])
                    yield
                    ps, PB = bank()
                    S.add("tensor", mm(ps[0:64, 0:64], a64("A0"), I64), reads=[ub["A0"], CST], writes=[PB])
                    S.add("scalar", act(a64("B0"), ps[0:64, 0:64], AF.Copy), reads=[PB], writes=[ub["B0"]])
                    S.add("vector", tt(a64("X0"), ps[0:64, 0:64], I64, ALU.add), reads=[PB, CST], writes=[ub["X0"]])
                    for lvl in range(1, 6):
                        cu, nx = str((lvl - 1) % 2), str(lvl % 2)
                        yield
                        ps, PB = bank()
                        S.add("tensor", mm(ps[0:64, 0:64], a64("B" + cu), a64("A" + cu)), reads=[ub["A" + cu], ub["B" + cu]], writes=[PB])
                        S.add("scalar", act(a64("A" + nx), ps[0:64, 0:64], AF.Copy), reads=[PB], writes=[ub["A" + nx]])
                        if lvl < 5:
                            yield
                            ps, PB = bank()
                            S.add("tensor", mm(ps[0:64, 0:64], a64("A" + cu), a64("B" + cu)), reads=[ub["A" + cu], ub["B" + cu]], writes=[PB])
                            S.add("vector", cp(a64("B" + nx), ps[0:64, 0:64]), reads=[PB], writes=[ub["B" + nx]])
                        yield
                        ps, PB = bank()
                        S.add("tensor", mm(ps[0:64, 0:64], a64("A" + nx), a64("X" + cu)), reads=[ub["A" + nx], ub["X" + cu]], writes=[PB])
                        S.add("vector", tt(a64("X" + nx), a64("X" + cu), ps[0:64, 0:64], ALU.add), reads=[PB, ub["X" + cu]], writes=[ub["X" + nx]])
                    XT, XB = a64("X1"), ub["X1"]
                    yield
                    ps, PB = bank()
                    S.add("tensor", mm(ps[0:64, 0:128], kT, I128()), reads=[HB[hs], CST], writes=[PB])
                    S.add("scalar", act(a64w("KBG"), ps[0:64, 0:128], AF.Copy, scale=sc(BEG)), reads=[PB, TM], writes=[ub["KBG"]])
                    S.add("vector", ts(a64w("KT"), ps[0:64, 0:128], sc(EKT), None, ALU.mult), reads=[PB, TM], writes=[ub["KT"]])
                    yield
                    ps, PB = bank()
                    S.add("tensor", mm(ps[0:64, 0:128], vT, I128()), reads=[HB[hs], CST], writes=[PB])
                    S.add("scalar", act(a64w("VB"), ps[0:64, 0:128], AF.Copy, scale=sc(BETA)), reads=[PB, TM], writes=[ub["VB"]])
                    yield
                    ps, PB = bank()
                    S.add("tensor", mm(ps[:, 0:64], a64w("KBG"), XT), reads=[ub["KBG"], XB], writes=[PB])
                    S.add("scalar", act(a128("WDT"), ps[:, 0:64], AF.Copy), reads=[PB], writes=[ub["WDT"]])
                    yield
                    ps, PB = bank()
                    S.add("tensor", mm(ps[0:64, 0:128], XT, a64w("VB")), reads=[ub["VB"], XB], writes=[PB])
                    S.add("vector", cp(a64w("UB"), ps[0:64, 0:128]), reads=[PB], writes=[ub["UB"]])
                    yield
                    ps, PB = bank()
                    S.add("tensor", mm(ps[0:64, 0:128], a128("WDT"), Sst[:, h, :]), reads=[ub["WDT"], SH[h]], writes=[PB])
                    S.add("vector", tt(a64w("U"), a64w("UB"), ps[0:64, 0:128], ALU.subtract), reads=[PB, ub["UB"]], writes=[ub["U"]])
                    if full:
                        yield
                        ps, PB = bank()
                        S.add("tensor", mm(ps[0:64, 0:128], a128("QDT"), Sst[:, h, :], True, False), reads=[ub["QDT"], SH[h]], writes=[PB])
                        S.add("tensor", mm(ps[0:64, 0:128], a64("AQT"), a64w("U"), False, True), reads=[ub["AQT"], ub["U"]], writes=[PB])
                        S.add("scalar", act(a64w("ON"), ps[0:64, 0:128], AF.Square, accum_out=ssq[:, us, 0:1]), reads=[PB], writes=[ub["ON"], ub["ssq"]])
                        S.add("scalar", act(ssq[:, us, 1:2], ssq[:, us, 0:1], AF.Sqrt, bias=EPS, scale=1.0 / 128), reads=[ub["ssq"]], writes=[ub["ssq"]])
                        S.add("vector", recip(ssq[:, us, 1:2], ssq[:, us, 1:2]), reads=[ub["ssq"]], writes=[ub["ssq"]])
                        S.add("vector", ts(a64w("ON"), ps[0:64, 0:128], ssq[:, us, 1:2], None, ALU.mult), reads=[PB, ub["ssq"], ub["ON"]], writes=[ub["ON"]])
                    yield
                    ps2, PB2 = bank()
                    S.add("tensor", mm(ps2[:, 0:128], a64w("KT"), a64w("U")), reads=[ub["KT"], ub["U"]], writes=[PB2])
                    S.add("vector", stt(Sst[:, h, :], Sst[:, h, :], tm[:, EGL, c, h:h + 1], ps2[:, 0:128], ALU.mult, ALU.add),
                          reads=[PB2, TM, SH[h]], writes=[SH[h]])
                    if full:
                        yield
                        ps, PB = bank()
                        S.add("tensor", mm(ps[:, 0:64], a64w("ON"), I64), reads=[ub["ON"], CST], writes=[PB])
                        S.add("scalar", act(yb[:, hs, cs], ps[:, 0:64], AF.Copy, scale=vecs[:, V_DNG:V_DNG + 1]), reads=[PB, VEC], writes=[YBB[hs]])


            G = 4
            for hg in range(0, NH, G):
                for h in range(hg, hg + G):
                    prep(h, h - hg)
                for c in range(8):
                    gens = [unit(h, c, h - hg, h - hg) for h in range(hg, hg + G)]
                    while gens:
                        for g_ in list(gens):
                            try:
                                next(g_)
                            except StopIteration:
                                gens.remove(g_)
                for h in range(hg, hg + G):
                    hs = h - hg
                    if full:
                        S.add("scalar", dma(YB[h, :, c0:c0 + T], yb[:, hs, :]), reads=[YBB[hs]], writes=[Buf("t")], kind="d")
            S.emit(block)

    def stageM():
        with (sbt("sm_mem", [128, 32, 256], F32) as memT, sbt("sm_mn", [128, 32, 256], BF16) as mn,
              sbt("sm_ring", [128, NSLOT, 32, 128], BF16) as ringt, sbt("sm_sq", [128, 2, 256], F32) as sqt,
              sbt("sm_rb", [128, 256], F32) as rb, sbt("sm_vmT", [128, 4, 256], BF16) as vmT,
              nc.Block() as block):
            MB = [Buf("mem%d" % k) for k in range(32)]
            MNB = [Buf("mn%d" % k) for k in range(32)]
            RBB, VMT = Buf("rbm"), Buf("vmT")
            ring, sq = Ring(ringt, NSLOT, "ringm"), Ring(sqt, 2, "sqm")
            for kc in range(32):
                S.add("sync", dma(memT[:, kc, :], memT_d[:, kc, :]), writes=[MB[kc]], kind="d")
            rmsnorm(lambda kc: memT[:, kc, :], lambda kc: MB[kc], 3, lambda kc: mn[:, kc, :], lambda kc: MNB[kc], 256, sq, rb, RBB)
            for c in range(4):
                ps, PB = linear(ring, "xa", 4 + c, lambda kc: mn[:, kc, :], lambda kc: MNB[kc], 256)
                S.add("scalar", act(KmT[:, c, :], ps[:, 0:256], AF.Copy), reads=[PB], writes=[KMT])
            for c in range(4):
                ps, PB = linear(ring, "xa", 8 + c, lambda kc: mn[:, kc, :], lambda kc: MNB[kc], 256)
                S.add("scalar", act(vmT[:, c, :], ps[:, 0:256], AF.Copy), reads=[PB], writes=[VMT])
            for c in range(4):
                for mc in range(2):
                    ps, PB = bank()
                    S.add("tensor", mm(ps[:, 0:128], vmT[:, c, mc * 128:(mc + 1) * 128], ibf[:, :]), reads=[VMT, IBF], writes=[PB])
                    S.add("vector", cp(Vm[:, mc, c * 128:(c + 1) * 128], ps[:, 0:128]), reads=[PB], writes=[VM])
            S.emit(block)

    def stage4a(mt):
        c0 = mt * T
        with (sbt("a_u", [128, 32, T], BF16) as uT, sbt("a_ring", [128, NSLOT, 32, 128], BF16) as ringt,
              sbt("a_yA", [128, 16, T], BF16) as yA, sbt("a_yB", [128, 16, T], BF16) as yB,
              sbt("a_mrg", [128, 32, T], BF16) as mrg, sbt("a_pcc", [128, 2, 2 + T], F32) as pcc,
              sbt("a_cv", [128, T], F32) as cv, sbt("a_ybl", [128, 2, T], F32) as ybl,
              sbt("a_sz", [128, 2, T], F32) as szt, sbt("a_m", [128, 2, T], F32) as mt_,
              sbt("a_h", [128, 2, T], F32) as hch, nc.Block() as block):
            UBF = [Buf("u%d" % k) for k in range(32)]
            YAB = [Buf("ya%d" % k) for k in range(16)]
            YBB = [Buf("yb%d" % k) for k in range(16)]
            MRB = [Buf("mr%d" % k) for k in range(32)]
            CV = Buf("cva")
            ring = Ring(ringt, NSLOT, "ringa")
            pc, yl, sz, mm_, hc = Ring(pcc, 2, "pc"), Ring(ybl, 2, "yl"), Ring(szt, 2, "sz"), Ring(mt_, 2, "m"), Ring(hch, 2, "hc")
            for kc in range(32):
                S.add("sync", dma(uT[:, kc, :], U1[:, kc, c0:c0 + T]), writes=[UBF[kc]], kind="d")
            ur, ubf = (lambda kc: uT[:, kc, :]), (lambda kc: UBF[kc])
            for i in range(16):
                ps, PB = linear(ring, "in4", i, ur, ubf, T)
                k = pc.next()
                S.add("sync", dma(pcc[:, k, :], PCC[i, :, c0:c0 + 2 + T]), writes=[pc.b[k]], kind="d")
                wv = lambda j, i=i: vecs[:, V_CA + i * 3 + j:V_CA + i * 3 + j + 1]
                S.add("vector", ts(cv[:, :], pcc[:, k, 0:T], wv(0), None, ALU.mult), reads=[pc.b[k], VEC], writes=[CV])
                for j in (1, 2):
                    S.add("vector", stt(cv[:, :], pcc[:, k, j:j + T], wv(j), cv[:, :], ALU.mult, ALU.add), reads=[pc.b[k], VEC, CV], writes=[CV])
                S.add("vector", tt(yA[:, i, :], cv[:, :], ps[:, 0:T], ALU.mult), reads=[CV, PB], writes=[YAB[i]])
            for h in range(16):
                ps, PB = linear(ring, "in4", 16 + h, ur, ubf, T)
                k = sz.next()
                S.add("scalar", act(szt[:, k, :], ps[:, 0:T], AF.Silu), reads=[PB], writes=[sz.b[k]])
                k2 = yl.next()
                S.add("sync", dma(ybl[:, k2, :], YB[h, :, c0:c0 + T]), writes=[yl.b[k2]], kind="d")
                S.add("vector", tt(yB[:, h, :], ybl[:, k2, :], szt[:, k, :], ALU.mult), reads=[yl.b[k2], sz.b[k]], writes=[YBB[h]])
            for oc in range(32):
                ps, PB = linear(ring, "in4", 32 + oc, ur, ubf, T)
                k = sz.next()
                S.add("scalar", act(szt[:, k, :], ps[:, 0:T], AF.Sigmoid), reads=[PB], writes=[sz.b[k]])
                ps, PB = linear(ring, "oc", oc, lambda kc: yA[:, kc, :], lambda kc: YAB[kc], T)
                ka = mm_.next()
                S.add("vector", tt(mt_[:, ka, :], szt[:, k, :], ps[:, 0:T], ALU.mult), reads=[sz.b[k], PB], writes=[mm_.b[ka]])
                ps, PB = linear(ring, "in4", 64 + oc, ur, ubf, T)
                k = sz.next()
                S.add("scalar", act(szt[:, k, :], ps[:, 0:T], AF.Sigmoid), reads=[PB], writes=[sz.b[k]])
                ps, PB = linear(ring, "od", oc, lambda kc: yB[:, kc, :], lambda kc: YBB[kc], T)
                kb = mm_.next()
                S.add("vector", tt(mt_[:, kb, :], szt[:, k, :], ps[:, 0:T], ALU.mult), reads=[sz.b[k], PB], writes=[mm_.b[kb]])
                S.add("vector", tt(mrg[:, oc, :], mt_[:, ka, :], mt_[:, kb, :], ALU.add), reads=[mm_.b[ka], mm_.b[kb]], writes=[MRB[oc]])
            for oc in range(32):
                ps, PB = linear(ring, "wo", oc, lambda kc: mrg[:, kc, :], lambda kc: MRB[kc], T)
                k = hc.next()
                S.add("sync", dma(hch[:, k, :], H1[:, oc, c0:c0 + T]), writes=[hc.b[k]], kind="d")
                S.add("vector", tt(hch[:, k, :], hch[:, k, :], ps[:, 0:T], ALU.add), reads=[hc.b[k], PB], writes=[hc.b[k]])
                S.add("scalar", dma(H1[:, oc, c0:c0 + T], hch[:, k, :]), reads=[hc.b[k]], writes=[Buf("t")], kind="d")
            S.emit(block)

    def stage4b(mt):
        c0 = mt * T
        SC = 128.0 ** -0.5
        with (sbt("b_hT", [128, 32, T], F32) as hT, sbt("b_xn", [128, 32, T], BF16) as xn,
              sbt("b_hid", [128, 22, T], BF16) as hid, sbt("b_ring", [128, 4, 32, 128], BF16) as ringt,
              sbt("b_sq", [128, 2, T], F32) as sqt, sbt("b_sg", [128, 2, T], F32) as sgt,
              sbt("b_rb", [128, T], F32) as rb, sbt("b_stg", [128, 2, T], F32) as stgt,
              sbt("b_qx", [128, 4, T], BF16) as qx, sbt("b_ox", [128, 4, T], BF16) as ox,
              sbt("b_pf", [128, 2, 256], F32) as pft, sbt("b_pn", [128, 2, 256], BF16) as pnt,
              sbt("b_pT", [128, 2, 2, 128], BF16) as pTt, sbt("b_st", [128, 2, 4], F32) as stt_,
              nc.Block() as block):
            HTB = [Buf("hT%d" % k) for k in range(32)]
            XNB = [Buf("xn%d" % k) for k in range(32)]
            HIDB = [Buf("hid%d" % k) for k in range(22)]
            QXB = [Buf("qx%d" % k) for k in range(4)]
            OXB = [Buf("ox%d" % k) for k in range(4)]
            RBB = Buf("rbb")
            ring = Ring(ringt, 4, "ringb")
            sq, sg, stg = Ring(sqt, 2, "sq"), Ring(sgt, 2, "sg"), Ring(stgt, 2, "stg")
            pf, pn, pT, st = Ring(pft, 2, "pf"), Ring(pnt, 2, "pn"), Ring(pTt, 2, "pT"), Ring(stt_, 2, "st")
            for kc in range(32):
                S.add("sync", dma(hT[:, kc, :], H1[:, kc, c0:c0 + T]), writes=[HTB[kc]], kind="d")
            hr, hb = (lambda kc: hT[:, kc, :]), (lambda kc: HTB[kc])
            xr, xb = (lambda kc: xn[:, kc, :]), (lambda kc: XNB[kc])
            rmsnorm(hr, hb, 2, xr, xb, T, sq, rb, RBB)
            for c in range(4):
                ps, PB = linear(ring, "xa", c, xr, xb, T)
                S.add("scalar", act(qx[:, c, :], ps[:, 0:T], AF.Copy), reads=[PB], writes=[QXB[c]])
            for hh in range(4):
                for tq in range(4):
                    tsl = slice(tq * 128, (tq + 1) * 128)
                    ps, PB = bank()
                    S.add("tensor", mm(ps[:, 0:256], qx[:, hh, tsl], KmT[:, hh, :]), reads=[QXB[hh], KMT], writes=[PB])
                    k = st.next()
                    S.add("vector", lambda e, ps=ps, k=k: e.tensor_reduce(out=stt_[:, k, 0:1], in_=ps[:, 0:256], axis=mybir.AxisListType.X, op=ALU.max),
                          reads=[PB], writes=[st.b[k]])
                    S.add("vector", ts(stt_[:, k, 1:2], stt_[:, k, 0:1], -SC, None, ALU.mult), reads=[st.b[k]], writes=[st.b[k]])
                    kf = pf.next()
                    S.add("scalar", act(pft[:, kf, :], ps[:, 0:256], AF.Exp, bias=stt_[:, k, 1:2], scale=SC, accum_out=stt_[:, k, 2:3]),
                          reads=[PB, st.b[k]], writes=[pf.b[kf], st.b[k]])
                    S.add("vector", recip(stt_[:, k, 3:4], stt_[:, k, 2:3]), reads=[st.b[k]], writes=[st.b[k]])
                    kn = pn.next()
                    S.add("vector", ts(pnt[:, kn, :], pft[:, kf, :], stt_[:, k, 3:4], None, ALU.mult), reads=[pf.b[kf], st.b[k]], writes=[pn.b[kn]])
                    kt = pT.next()
                    for mc in range(2):
                        ps2, PB2 = bank()
                        S.add("tensor", mm(ps2[:, 0:128], pnt[:, kn, mc * 128:(mc + 1) * 128], ibf[:, :]), reads=[pn.b[kn], IBF], writes=[PB2])
                        S.add("scalar", act(pTt[:, kt, mc, :], ps2[:, 0:128], AF.Copy), reads=[PB2], writes=[pT.b[kt]])
                    ps3, PB3 = bank()
                    S.add("tensor", mm(ps3[:, 0:128], Vm[:, 0, hh * 128:(hh + 1) * 128], pTt[:, kt, 0, :], True, False), reads=[VM, pT.b[kt]], writes=[PB3])
                    S.add("tensor", mm(ps3[:, 0:128], Vm[:, 1, hh * 128:(hh + 1) * 128], pTt[:, kt, 1, :], False, True), reads=[VM, pT.b[kt]], writes=[PB3])
                    S.add("vector", cp(ox[:, hh, tsl], ps3[:, 0:128]), reads=[PB3], writes=[OXB[hh]])
            for oc in range(32):
                ps, PB = linear(ring, "xo", oc, lambda kc: ox[:, kc, :], lambda kc: OXB[kc], T)
                S.add("vector", tt(hT[:, oc, :], hT[:, oc, :], ps[:, 0:T], ALU.add), reads=[PB, HTB[oc]], writes=[HTB[oc]])
            rmsnorm(hr, hb, 4, xr, xb, T, sq, rb, RBB)
            ffn(ring, "f2", xn, XNB, hT, HTB, hid, HIDB, sg)
            SB_ = [None]

            def fdst(kc):
                SB_[0] = stg.next()
                return stgt[:, SB_[0], :]

            ps, PB = bank()
            for kc in range(32):
                i = sq.next()
                S.add("scalar", act(sqt[:, i, :], hT[:, kc, :], AF.Square), reads=[HTB[kc]], writes=[sq.b[i]])
                S.add("tensor", mm(ps[:, 0:T], cst[:, C_ONE:C_ONE + 128], sqt[:, i, :], kc == 0, kc == 31),
                      reads=[sq.b[i], CST], writes=[PB] if kc in (0, 31) else [])
            S.add("scalar", act(rb[:, :], ps[:, 0:T], AF.Sqrt, bias=EPS, scale=1.0 / D), reads=[PB], writes=[RBB])
            S.add("vector", recip(rb[:, :], rb[:, :]), reads=[RBB], writes=[RBB])
            for kc in range(32):
                i = stg.next()
                S.add("vector", stt(stgt[:, i, :], hT[:, kc, :], gain(5, kc), rb[:, :], ALU.mult, ALU.mult),
                      reads=[HTB[kc], RBB, VEC], writes=[stg.b[i]])
                S.add("scalar", dma(yT_d[:, kc, c0:c0 + T], stgt[:, i, :]), reads=[stg.b[i]], writes=[Buf("t")], kind="d")
            S.emit(block)

    PAIRS = [list(range(8))]
    XI1 = nc.dram_tensor("XI1", [128, 176], F32, kind="Internal").ap()
    XO1 = nc.dram_tensor("XO1", [8 * 128, 176], F32, kind="Internal").ap()
    XI2 = nc.dram_tensor("XI2", [128, 2048], F32, kind="Internal").ap()
    XO2 = nc.dram_tensor("XO2", [8 * 128, 2048], F32, kind="Internal").ap()
    def oh(r):
        return vecs[:, V_OH + r:V_OH + r + 1]

    def exch1():
        tl = nmt * T
        with (sbt("x1a", [128, 176], F32) as hx, sbt("x1b", [128, 176], F32) as hx2,
              sbt("x1c", [128, 8, 176], F32) as hx8, nc.Block() as block):
            HX, HX2, BXI, BXO = Buf("hx"), Buf("hx2"), Buf("xi1"), Buf("xo1")
            for c in range(48):
                S.add("sync", dma(hx[:, 3 * c:3 * c + 3], QKV[c, :, tl:tl + 3]), writes=[HX], reads=[], kind="d")
                HX.writer = None
            for c in range(16):
                S.add("sync", dma(hx[:, 144 + 2 * c:146 + 2 * c], PCC[c, :, tl:tl + 2]), writes=[HX], reads=[], kind="d")
                HX.writer = None
            loads = S.ops["sync"][-64:]
            st_ = S.add("sync", dma(XI1, hx[:, :]), reads=[], writes=[BXI], kind="d")
            for d_ in loads:
                d_.signal = True
                st_.deps.append(d_)
            S.add("gpsimd", lambda e: e.collective_compute("AllGather", ALU.bypass, replica_groups=PAIRS, ins=[XI1], outs=[XO1]),
                  reads=[BXI], writes=[BXO], kind="cc")
            HX8 = Buf("hx8")
            S.add("sync", dma(hx8[:, :, :], XO1.rearrange("(r p) n -> p r n", p=128)), reads=[BXO], writes=[HX8], kind="d")
            S.add("vector", ts(hx2[:, :], hx8[:, 0, :], oh(0), None, ALU.mult), reads=[HX8, VEC], writes=[HX2])
            for r in range(1, 8):
                S.add("vector", stt(hx2[:, :], hx8[:, r, :], oh(r), hx2[:, :], ALU.mult, ALU.add), reads=[HX8, VEC, HX2], writes=[HX2])
            for c in range(48):
                S.add("sync", dma(QKV[c, :, 0:3], hx2[:, 3 * c:3 * c + 3]), reads=[HX2], writes=[Buf("t")], kind="d")
            for c in range(16):
                S.add("sync", dma(PCC[c, :, 0:2], hx2[:, 144 + 2 * c:146 + 2 * c]), reads=[HX2], writes=[Buf("t")], kind="d")
            S.emit(block)

    def exch2():
        with sbt("x2", [128, 8, 2048], F32) as sx, nc.Block() as block:
            SX, BXI, BXO = Buf("sx"), Buf("xi2"), Buf("xo2")
            S.add("sync", dma(XI2, Sst[:].rearrange("p a b -> p (a b)")), reads=[SST], writes=[BXI], kind="d")
            S.add("gpsimd", lambda e: e.collective_compute("AllGather", ALU.bypass, replica_groups=PAIRS, ins=[XI2], outs=[XO2]),
                  reads=[BXI], writes=[BXO], kind="cc")
            for r in range(8):
                S.add("sync", dma(sx[:, r, :], XO2[r * 128:(r + 1) * 128, :]), reads=[BXO], writes=[SX], kind="d")
                SX.writer = None
            lds = S.ops["sync"][-8:]
            sflat = Sst[:].rearrange("p a b -> p (a b)")
            op = S.add("vector", ts(sflat, sx[:, 0, :], oh(0), None, ALU.mult), reads=[VEC, SST], writes=[SST])
            for d_ in lds:
                d_.signal = True
                op.deps.append(d_)
            for r in range(1, 8):
                S.add("vector", stt(sflat, sx[:, r, :], oh(r), sflat, ALU.mult, ALU.add), reads=[VEC, SST], writes=[SST])
            S.emit(block)

    if "1" in stages:
        for mt in range(nmt):
            stage1(mt)
    if "2" in stages:
        if "x" in stages:
            exch1()
        for mt in range(nmt):
            stage2(mt, full=("x" not in stages))
        if "x" in stages:
            exch2()
            for mt in range(nmt):
                stage2(mt)
    if "M" in stages:
        stageM()
    for mt in range(nmt):
        if "4a" in stages:
            stage4a(mt)
        if "4b" in stages:
            stage4b(mt)
    if debug:
        with sbt("dbg", [128, 4096], F32) as dbg, nc.Block() as block:
            DB_ = Buf("dd")
            S.add("vector", cp(dbg[:, 0:1024], KmT[:].rearrange("p a b -> p (a b)")), writes=[DB_])
            S.add("vector", cp(dbg[:, 1024:2048], Vm[:].rearrange("p a b -> p (a b)")), writes=[DB_])
            S.add("vector", cp(dbg[:, 2048:4096], Sst[:, 0:16, :].rearrange("p a b -> p (a b)")), writes=[DB_])
            S.add("sync", dma(DBG, dbg[:]), reads=[DB_], writes=[Buf("d2")], kind="d")
            S.emit(block)
    return nc


_CACHE = {}


def kernel(**inputs):
    inp = {k: np.asarray(v) for k, v in inputs.items()}
    shards = _prep_weights(inp)
    vecs, consts = _prep_small(inp)
    x = inp["x"].astype(np.float32)
    mem = inp["mem"].astype(np.float32)
    in_maps = []
    for c in range(8):
        b, half = c // 2, c % 2
        m = dict(shards[c])
        xs = x[b, half * NTOK:(half + 1) * NTOK]
        m["xT"] = np.ascontiguousarray(xs.T.reshape(32, 128, NTOK).transpose(1, 0, 2))
        m["memT"] = np.ascontiguousarray(mem[b].T.reshape(32, 128, 256).transpose(1, 0, 2))
        vc = vecs.copy()
        if half == 1:
            vc[:, V_OH + c - 1] = 1.0
        m["vecs"] = vc
        m["consts"] = consts
        in_maps.append(m)
    if "nc" not in _CACHE:
        _CACHE["nc"] = build()
    res = run_bass_kernel_spmd(_CACHE["nc"], in_maps, core_ids=list(range(8)))
    out = np.empty((4, 2 * NTOK, D), np.float32)
    for c in range(8):
        b, half = c // 2, c % 2
        yT = np.asarray(res.results[c]["yT"])
        out[b, half * NTOK:(half + 1) * NTOK] = yT.transpose(2, 1, 0).reshape(NTOK, D)
    return out
```

```python
import numpy as np
import concourse.bass as bass
import concourse.mybir as mybir

F32 = mybir.dt.float32
BF16 = mybir.dt.bfloat16
AF = mybir.ActivationFunctionType
ALU = mybir.AluOpType

ENGS = ["tensor", "vector", "scalar", "gpsimd", "sync"]


class Buf:
    __slots__ = ("name", "writer", "readers", "excl")

    def __init__(self, name, excl=False):
        self.name = name
        self.writer = None
        self.readers = {}
        self.excl = excl


class Op:
    __slots__ = ("eng", "fn", "deps", "signal", "sem", "val", "kind", "epoch")

    def __init__(self, eng, fn, kind):
        self.eng = eng
        self.fn = fn
        self.kind = kind
        self.deps = []
        self.signal = False
        self.sem = None
        self.val = 0
        self.epoch = 0


class Sched:
    def __init__(self, nc, ndma=None):
        ndma_q = {"gpsimd": 6, "sync": 24, "scalar": 8}
        self.nc = nc
        self.ops = {e: [] for e in ENGS}
        self.sems = {e: nc.semaphore("sem_" + e).__enter__() for e in ENGS}
        self.cnt = {e: 0 for e in ENGS}
        self.dsems = {e: [nc.semaphore("dsem_%s_%d" % (e, i)).__enter__() for i in range(ndma_q[e])]
                      for e in ("gpsimd", "sync", "scalar")}
        self.dcnt = {e: [0] * ndma_q[e] for e in ("gpsimd", "sync", "scalar")}
        self.drr = {"gpsimd": 0, "sync": 0, "scalar": 0}
        self.waited = {e: {} for e in ENGS}
        self.pending = []
        self.alldma = []
        self.last = {e: None for e in ENGS}
        self.ccs = []
        self.epoch = 0
        self.nsem = 0

    def add(self, eng, fn, reads=(), writes=(), kind="c"):
        op = Op(eng, fn, kind)
        op.epoch = self.epoch
        deps = []

        def dep(o):
            if o is None or o is op or (o.epoch < self.epoch and o.kind != "cc"):
                return
            if o.eng == eng and o.kind == "c" and kind == "c" and eng == "tensor":
                return
            deps.append(o)

        for b in reads:
            if b.excl:
                if b.writer is not None and b.writer.eng != eng:
                    dep(b.writer)
                for e2, o in b.readers.items():
                    if e2 != eng:
                        dep(o)
            else:
                dep(b.writer)
        for b in writes:
            if b.excl:
                if b.writer is not None and b.writer.eng != eng:
                    dep(b.writer)
                for e2, o in b.readers.items():
                    if e2 != eng:
                        dep(o)
            else:
                dep(b.writer)
                for o in b.readers.values():
                    dep(o)
        for b in reads:
            if kind in ("d", "cc"):
                b.readers[("dma", id(op))] = op
            else:
                b.readers[eng] = op
        for b in writes:
            b.writer = op
            b.readers = {}
        seen = set()
        for d in deps:
            if id(d) not in seen:
                seen.add(id(d))
                d.signal = True
                op.deps.append(d)
        self.ops[eng].append(op)
        self.pending.append(op)
        if kind in ("d", "cc"):
            self.alldma.append(op)
        self.last[eng] = op
        return op

    def barrier(self):
        lasts = [o for o in self.last.values() if o is not None and o.kind == "c"]
        dmas = [d for d in self.alldma if d.kind != "cc"]
        self.alldma = []
        for e in ENGS:
            op = Op(e, None, "w")
            op.epoch = self.epoch
            for d in lasts + dmas:
                if d.eng == e and d.kind == "c":
                    continue
                d.signal = True
                op.deps.append(d)
            self.ops[e].append(op)
            self.pending.append(op)

    def emit(self, block):
        nc = self.nc
        self.barrier()
        self.epoch += 1
        pend = self.pending
        self.pending = []
        for e in ENGS:
            if self.cnt[e] > 1500:
                self.nsem += 1
                self.sems[e] = nc.semaphore("sem_%s_%d" % (e, self.nsem)).__enter__()
                self.cnt[e] = 0
        for op in pend:
            if op.kind == "c":
                if op.signal:
                    self.cnt[op.eng] += 1
                    op.sem = self.sems[op.eng]
                    op.val = self.cnt[op.eng]
            elif op.kind == "d":
                i = self.drr[op.eng]
                self.drr[op.eng] = (i + 1) % len(self.dsems[op.eng])
                self.dcnt[op.eng][i] += 16
                op.sem = self.dsems[op.eng][i]
                op.val = self.dcnt[op.eng][i]
            elif op.kind == "cc":
                s = nc.semaphore("ccsem%d" % len(self.ccs)).__enter__()
                self.ccs.append(s)
                op.sem = s
                op.val = 1
        per = {e: [o for o in pend if o.eng == e] for e in ENGS}

        def make(e):
            def body(eng):
                w = self.waited[e]
                for op in per[e]:
                    for d in op.deps:
                        k = id(d.sem)
                        if w.get(k, 0) < d.val:
                            eng.wait_ge(d.sem, d.val)
                            w[k] = d.val
                    if op.fn is None:
                        continue
                    ins = op.fn(eng)
                    if op.kind == "d":
                        ins.then_inc(op.sem, 16)
                    elif op.kind == "cc":
                        ins.then_inc(op.sem, 1)
                    elif op.signal:
                        ins.then_inc(op.sem, 1)
            return body

        for e in ENGS:
            if per[e]:
                getattr(block, e)(make(e))

from concourse.bass_utils import run_bass_kernel_spmd

D = 4096
FF = 11008
NTOK = 1024
T = 512
NMT = NTOK // T
NH = 16
EPS = 1e-6
HG = [(0, 22), (22, 44), (44, 65), (65, 86)]
NSLOT = 6
PSZ = 4

GROUPS = {
    "f1gu": (176, 32), "f1d": (128, 22), "in1": (88, 32), "in4": (96, 32), "oc": (32, 16),
    "od": (32, 16), "wo": (32, 32), "xa": (16, 32), "xo": (32, 4), "f2gu": (176, 32), "f2d": (128, 22),
}
GORDER = ["f1gu", "f1d", "in1", "in4", "oc", "od", "wo", "xa", "xo", "f2gu", "f2d"]

V_GAIN = 0
V_CA = 192
V_CQ = 240
V_ALOG = 432
V_DTB = 448
V_DNG = 464
V_OH = 468
NV = 476
C_I, C_ONE, C_NEG, C_TRI, C_UPP, C_NMS, C_NMT = 0, 128, 256, 384, 448, 512, 576
NCC = 640


def _mkchunk(Wsub, kc):
    K, n = Wsub.shape
    out = np.zeros((128, kc, 128), np.float32)
    k = K // 128
    out[:, :k, :n] = Wsub.reshape(k, 128, n).transpose(1, 0, 2)
    return out


def _prep_weights(inp):
    g = {}
    def ffn(pre, wg, wu, wd):
        gu = []
        for j in range(86):
            gu.append(_mkchunk(wg[:, j * 128:(j + 1) * 128], 32))
            gu.append(_mkchunk(wu[:, j * 128:(j + 1) * 128], 32))
        g[pre + "gu"] = gu
        dd = []
        for (a, b) in HG:
            for oc in range(32):
                dd.append(_mkchunk(wd[a * 128:b * 128, oc * 128:(oc + 1) * 128], 22))
        g[pre + "d"] = dd
    ffn("f1", inp["ffn1_w_gate"][0], inp["ffn1_w_up"][0], inp["ffn1_w_down"][0])
    ffn("f2", inp["ffn2_w_gate"][0], inp["ffn2_w_up"][0], inp["ffn2_w_down"][0])
    wi = inp["w_in"][0]
    c1 = []
    for base in (6144, 8192, 10240):
        for i in range(16):
            c1.append(_mkchunk(wi[:, base + 128 * i: base + 128 * (i + 1)], 32))
    c1.append(_mkchunk(wi[:, 14336:14368], 32))
    for i in range(16):
        c1.append(_mkchunk(wi[:, 128 * i:128 * (i + 1)], 32))
        c1.append(_mkchunk(wi[:, 2048 + 128 * i:2048 + 128 * (i + 1)], 32))
    g["in1"] = c1
    c4 = []
    for base, n in ((4096, 16), (12288, 16), (14368, 32), (18464, 32)):
        for i in range(n):
            c4.append(_mkchunk(wi[:, base + 128 * i: base + 128 * (i + 1)], 32))
    g["in4"] = c4
    g["oc"] = [_mkchunk(inp["w_out_conv"][0][:, 128 * i:128 * (i + 1)], 16) for i in range(32)]
    g["od"] = [_mkchunk(inp["w_out_delta"][0][:, 128 * i:128 * (i + 1)], 16) for i in range(32)]
    g["wo"] = [_mkchunk(inp["w_o"][0][:, 128 * i:128 * (i + 1)], 32) for i in range(32)]
    g["xa"] = [_mkchunk(inp[k][0][:, 128 * i:128 * (i + 1)], 32)
               for k in ("xattn_wq", "xattn_wk", "xattn_wv") for i in range(4)]
    g["xo"] = [_mkchunk(inp["xattn_wo"][0][:, 128 * i:128 * (i + 1)], 4) for i in range(32)]
    shards = [dict() for _ in range(8)]
    for name, (n, kc) in GROUPS.items():
        ch = g[name]
        while len(ch) < n:
            ch.append(np.zeros((128, kc, 128), np.float32))
        for r in range(8):
            arr = np.stack([ch[l * 8 + r] for l in range(n // 8)], 0)
            shards[r]["w_" + name] = np.ascontiguousarray(arr.reshape(n // 8 * 128, kc * 128))
    return shards


def _prep_small(inp):
    vecs = np.zeros((128, NV), np.float32)
    for gi, k in enumerate(["ffn1_norm", "mix_norm", "xattn_norm", "mem_norm", "ffn2_norm", "final_norm"]):
        v = np.asarray(inp[k]).reshape(-1)
        vecs[:, V_GAIN + gi * 32:V_GAIN + (gi + 1) * 32] = v.reshape(32, 128).T
    cw = inp["conv_w"][0]
    vecs[:, V_CA:V_CA + 48] = cw.reshape(3, 16, 128).transpose(2, 1, 0).reshape(128, 48)
    qw = inp["qkv_conv_w"][0]
    vecs[:, V_CQ:V_CQ + 192] = qw.reshape(4, 48, 128).transpose(2, 1, 0).reshape(128, 192)
    vecs[:, V_ALOG:V_ALOG + 16] = inp["a_log"][0][None, :]
    vecs[:, V_DTB:V_DTB + 16] = inp["dt_bias"][0][None, :]
    vecs[:, V_DNG] = inp["dn_out_norm"][0]
    c = np.zeros((128, NCC), np.float32)
    c[:, C_I:C_I + 128] = np.eye(128)
    c[:, C_ONE:C_ONE + 128] = 1.0
    c[:, C_NEG:C_NEG + 128] = -1.0
    t = np.arange(64)
    c[:64, C_TRI:C_TRI + 64] = (t[:, None] <= t[None, :])
    c[:64, C_UPP:C_UPP + 64] = (t[:, None] > t[None, :])
    c[:64, C_NMS:C_NMS + 64] = np.where(t[:, None] > t[None, :], 0.0, -30000.0)
    c[:64, C_NMT:C_NMT + 64] = np.where(t[None, :] >= t[:, None], 0.0, -30000.0)
    return vecs, c


def build(debug=False, stages=("M", "1", "2", "x", "4a", "4b"), nmt=NMT, groups=None, ntok=NTOK):
    groups = list(GORDER) if groups is None else groups
    nc = bass.Bass("TRN2", target_bir_lowering=False)
    dk = "ExternalOutput" if debug else "Internal"
    xT_d = nc.dram_tensor("xT", [128, 32, ntok], F32, kind="ExternalInput").ap()
    memT_d = nc.dram_tensor("memT", [128, 32, 256], F32, kind="ExternalInput").ap()
    vecs_d = nc.dram_tensor("vecs", [128, NV], F32, kind="ExternalInput").ap()
    consts_d = nc.dram_tensor("consts", [128, NCC], F32, kind="ExternalInput").ap()
    yT_d = nc.dram_tensor("yT", [128, 32, ntok], F32, kind="ExternalOutput").ap()
    w_ext, w_cc, w_all = {}, {}, {}
    for name in groups:
        n, kc = GROUPS[name]
        w_ext[name] = nc.dram_tensor("w_" + name, [n // 8 * 128, kc * 128], F32, kind="ExternalInput").ap()
        w_cc[name] = nc.dram_tensor("cc_" + name, [n // 8 * 128, kc * 128], BF16, kind="Internal").ap()
        w_all[name] = []
        for p0 in range(0, n // 8, PSZ):
            psz = min(PSZ, n // 8 - p0)
            w_all[name].append(nc.dram_tensor("wall_%s_%d" % (name, p0), [8 * psz * 128, kc * 128], BF16, kind="Internal").ap())
    H1 = nc.dram_tensor("H1", [128, 32, ntok], F32, kind=dk).ap()
    U1 = nc.dram_tensor("U1", [128, 32, ntok], BF16, kind="Internal").ap()
    DBG = nc.dram_tensor("DBG", [128, 4096], F32, kind=dk).ap()
    QKV = nc.dram_tensor("QKV", [48, 128, 3 + ntok], F32, kind=dk).ap()
    BAp = nc.dram_tensor("BAp", [32, ntok], F32, kind=dk).ap()
    PCC = nc.dram_tensor("PCC", [16, 128, 2 + ntok], F32, kind=dk).ap()
    YB = nc.dram_tensor("YB", [16, 128, ntok], F32, kind=dk).ap()

    S = Sched(nc)
    ctx = []

    uniq = [0]

    def sbt(name, shape, dt):
        uniq[0] += 1
        return nc.sbuf_tensor("%s_%d" % (name, uniq[0]), shape, dt)

    pers = [
        nc.sbuf_tensor("vecs_s", [128, NV], F32), nc.sbuf_tensor("consts_s", [128, NCC], F32),
        nc.sbuf_tensor("ibf", [128, 128], BF16), nc.sbuf_tensor("Sst", [128, NH, 128], F32),
        nc.sbuf_tensor("negA", [128, 16], F32), nc.sbuf_tensor("KmT", [128, 4, 256], BF16),
        nc.sbuf_tensor("Vm", [128, 2, 512], BF16), nc.sbuf_tensor("zero", [128, 64], F32),
    ]
    vecs, cst, ibf, Sst, negA, KmT, Vm, zero = [p.__enter__() for p in pers]
    psum = [nc.psum_tensor("ps%d" % i, [128, 512], F32).__enter__() for i in range(8)]
    PSB = [Buf("ps%d" % i, True) for i in range(8)]
    pstate = [0]

    def bank():
        i = pstate[0]
        pstate[0] = (i + 1) % 8
        return psum[i], PSB[i]

    VEC, CST, IBF, SST, NEGA, KMT, VM, ZERO = [Buf(n) for n in ("vecs", "cst", "ibf", "sst", "nega", "kmt", "vm", "zero")]
    WALL = {g: [Buf("wall_%s_%d" % (g, p)) for p in range((GROUPS[g][0] // 8 + PSZ - 1) // PSZ)] for g in GROUPS}
    if debug:
        for _n in ("U1",):
            pass
    B_H1 = [Buf("H1_%d" % m) for m in range(NMT)]
    B_U1 = [Buf("U1_%d" % m) for m in range(NMT)]
    B_QKV, B_BA, B_PCC, B_YB, B_OUT = Buf("qkv"), Buf("ba"), Buf("pcc"), Buf("yb"), Buf("out")

    def I128():
        return cst[:, C_I:C_I + 128]

    def gain(gi, kc):
        return vecs[:, V_GAIN + gi * 32 + kc:V_GAIN + gi * 32 + kc + 1]

    def mm(o, l, r, st=True, sp=True):
        return lambda e: e.matmul(o, l, r, start=st, stop=sp)

    def act(o, i, f, **kw):
        return lambda e: e.activation(o, i, f, **kw)

    def dma(o, i):
        return lambda e: e.dma_start(out=o, in_=i)

    def tt(o, a, b, op):
        return lambda e: e.tensor_tensor(o, a, b, op)

    def stt(o, a, s, b, op0, op1):
        return lambda e: e.scalar_tensor_tensor(out=o, in0=a, scalar=s, in1=b, op0=op0, op1=op1)

    def ts(o, a, s1, s2, op0, op1=None):
        if op1 is None:
            return lambda e: e.tensor_scalar(o, a, s1, None, op0)
        return lambda e: e.tensor_scalar(o, a, s1, s2, op0, op1)

    def cp(o, i):
        return lambda e: e.tensor_copy(o, i)

    def recip(o, i):
        return lambda e: e.reciprocal(o, i)

    with nc.Block() as block:
        S.add("sync", dma(vecs[:], vecs_d), writes=[VEC], kind="d")
        S.add("sync", dma(cst[:], consts_d), writes=[CST], kind="d")
        S.add("vector", cp(ibf[:], cst[:, C_I:C_I + 128]), reads=[CST], writes=[IBF])
        S.add("vector", lambda e: e.memset(Sst[:], 0.0), writes=[SST])
        S.add("vector", lambda e: e.memset(zero[:], 0.0), writes=[ZERO])
        S.add("scalar", act(negA[:], vecs[:, V_ALOG:V_ALOG + 16], AF.Exp), reads=[VEC], writes=[NEGA])
        S.add("vector", ts(negA[:], negA[:], -1.0, None, ALU.mult), reads=[NEGA], writes=[NEGA])
        for c in range(48):
            S.add("sync", dma(QKV[c, :, 0:3], zero[:, 0:3]), reads=[ZERO], writes=[B_QKV], kind="d")
        for c in range(16):
            S.add("sync", dma(PCC[c, :, 0:2], zero[:, 0:2]), reads=[ZERO], writes=[B_PCC], kind="d")
        for g in groups:
            n, kc = GROUPS[g]
            for pi, p0 in enumerate(range(0, n // 8, PSZ)):
                psz = min(PSZ, n // 8 - p0)
                CCB = Buf("cc_%s_%d" % (g, pi))
                for l in range(p0, p0 + psz):
                    S.add("gpsimd", dma(w_cc[g][l * 128:(l + 1) * 128, :], w_ext[g][l * 128:(l + 1) * 128, :]), writes=[Buf("t")], reads=[], kind="d")
                    CCB.readers = {}
                lastd = S.ops["gpsimd"][-psz:]
                op = S.add("gpsimd", (lambda g=g, pi=pi, p0=p0, psz=psz: lambda e: e.collective_compute(
                    "AllGather", ALU.bypass, replica_groups=[list(range(8))],
                    ins=[w_cc[g][p0 * 128:(p0 + psz) * 128, :]], outs=[w_all[g][pi]]))(),
                    reads=[], writes=[WALL[g][pi]], kind="cc")
                for d_ in lastd:
                    d_.signal = True
                    op.deps.append(d_)
        S.emit(block)

    class Ring:
        def __init__(self, t, n, name):
            self.t, self.n, self.i = t, n, 0
            self.b = [Buf("%s%d" % (name, k)) for k in range(n)]

        def next(self):
            i = self.i
            self.i = (i + 1) % self.n
            return i

    def load_w(ring, g, idx):
        n, kc = GROUPS[g]
        l = idx // 8
        pi = l // PSZ
        psz = min(PSZ, n // 8 - pi * PSZ)
        row = (idx % 8) * psz + (l - pi * PSZ)
        s = ring.next()
        S.add("sync", dma(ring.t[:, s, 0:kc, :], w_all[g][pi][row * 128:(row + 1) * 128, :].rearrange("p (k n) -> p k n", n=128)),
              reads=[WALL[g][pi]], writes=[ring.b[s]], kind="d")
        return s

    def linear(ring, g, idx, rhs, rhsb, W, M=128, nk=None):
        kc_n = GROUPS[g][1] if nk is None else nk
        s = load_w(ring, g, idx)
        ps, PB = bank()
        for kc in range(kc_n):
            S.add("tensor", mm(ps[0:M, 0:W], ring.t[:, s, kc, 0:M], rhs(kc), kc == 0, kc == kc_n - 1),
                  reads=[ring.b[s], rhsb(kc)], writes=[PB] if kc in (0, kc_n - 1) else [])
        return ps, PB

    def rmsnorm(src, srcb, gi, dst, dstb, W, sq, rb, RBB, nk=32):
        ps, PB = bank()
        for kc in range(nk):
            i = sq.next()
            S.add("scalar", act(sq.t[:, i, 0:W], src(kc), AF.Square), reads=[srcb(kc)], writes=[sq.b[i]])
            S.add("tensor", mm(ps[:, 0:W], cst[:, C_ONE:C_ONE + 128], sq.t[:, i, 0:W], kc == 0, kc == nk - 1),
                  reads=[sq.b[i], CST], writes=[PB] if kc in (0, nk - 1) else [])
        S.add("scalar", act(rb[:, 0:W], ps[:, 0:W], AF.Sqrt, bias=EPS, scale=1.0 / D), reads=[PB], writes=[RBB])
        S.add("vector", recip(rb[:, 0:W], rb[:, 0:W]), reads=[RBB], writes=[RBB])
        for kc in range(nk):
            S.add("vector", stt(dst(kc), src(kc), gain(gi, kc), rb[:, 0:W], ALU.mult, ALU.mult),
                  reads=[srcb(kc), RBB, VEC], writes=[dstb(kc)])

    def ffn(ring, pre, xn, XNB, hT, HTB, hid, HIDB, sg):
        for gi_, (a, b) in enumerate(HG):
            for j in range(a, b):
                pg, PG = linear(ring, pre + "gu", 2 * j, lambda kc: xn[:, kc, :], lambda kc: XNB[kc], T)
                pu, PU = linear(ring, pre + "gu", 2 * j + 1, lambda kc: xn[:, kc, :], lambda kc: XNB[kc], T)
                i = sg.next()
                S.add("scalar", act(sg.t[:, i, :], pg[:, 0:T], AF.Silu), reads=[PG], writes=[sg.b[i]])
                S.add("vector", tt(hid[:, j - a, :], sg.t[:, i, :], pu[:, 0:T], ALU.mult),
                      reads=[sg.b[i], PU], writes=[HIDB[j - a]])
            for oc in range(32):
                pd, PD = linear(ring, pre + "d", gi_ * 32 + oc, lambda kc: hid[:, kc, :], lambda kc: HIDB[kc], T, nk=b - a)
                S.add("vector", stt(hT[:, oc, :], pd[:, 0:T], 0.5, hT[:, oc, :], ALU.mult, ALU.add),
                      reads=[PD, HTB[oc]], writes=[HTB[oc]])

    def stage1(mt):
        c0 = mt * T
        with (sbt("s1_hT", [128, 32, T], F32) as hT, sbt("s1_xn", [128, 32, T], BF16) as xn,
              sbt("s1_hid", [128, 22, T], BF16) as hid, sbt("s1_ring", [128, NSLOT, 32, 128], BF16) as ringt,
              sbt("s1_sq", [128, 2, T], F32) as sqt, sbt("s1_sg", [128, 2, T], F32) as sgt,
              sbt("s1_rb", [128, T], F32) as rb, sbt("s1_stg", [128, 4, T], F32) as stgt,
              nc.Block() as block):
            HTB = [Buf("hT%d" % k) for k in range(32)]
            XNB = [Buf("xn%d" % k) for k in range(32)]
            HIDB = [Buf("hid%d" % k) for k in range(22)]
            RBB = Buf("rb")
            ring = Ring(ringt, NSLOT, "ring")
            sq, sg, stg = Ring(sqt, 2, "sq"), Ring(sgt, 2, "sg"), Ring(stgt, 4, "stg")
            for kc in range(32):
                S.add("sync", dma(hT[:, kc, :], xT_d[:, kc, c0:c0 + T]), writes=[HTB[kc]], kind="d")
            rmsnorm(lambda kc: hT[:, kc, :], lambda kc: HTB[kc], 0, lambda kc: xn[:, kc, :], lambda kc: XNB[kc], T, sq, rb, RBB)
            import os
            lvl = int(os.environ.get("DBG_S1", "9"))
            if lvl >= 2:
                ffn(ring, "f1", xn, XNB, hT, HTB, hid, HIDB, sg)
            rmsnorm(lambda kc: hT[:, kc, :], lambda kc: HTB[kc], 1, lambda kc: xn[:, kc, :], lambda kc: XNB[kc], T, sq, rb, RBB)
            for kc in range(32):
                S.add("scalar", dma(H1[:, kc, c0:c0 + T], hT[:, kc, :]), reads=[HTB[kc]], writes=[Buf("t")], kind="d")
                S.add("scalar", dma(U1[:, kc, c0:c0 + T], xn[:, kc, :]), reads=[XNB[kc]], writes=[Buf("t")], kind="d")
            xr, xb = (lambda kc: xn[:, kc, :]), (lambda kc: XNB[kc])
            if lvl < 3:
                S.emit(block)
                return
            for c in range(48):
                ps, PB = linear(ring, "in1", c, xr, xb, T)
                i = stg.next()
                S.add("scalar", act(stg.t[:, i, :], ps[:, 0:T], AF.Copy), reads=[PB], writes=[stg.b[i]])
                S.add("scalar", dma(QKV[c, :, 3 + c0:3 + c0 + T], stg.t[:, i, :]), reads=[stg.b[i]], writes=[Buf("t")], kind="d")
            ps, PB = linear(ring, "in1", 48, xr, xb, T, M=32)
            i = stg.next()
            S.add("scalar", act(stg.t[0:32, i, :], ps[0:32, 0:T], AF.Copy), reads=[PB], writes=[stg.b[i]])
            S.add("scalar", dma(BAp[:, c0:c0 + T], stg.t[0:32, i, :]), reads=[stg.b[i]], writes=[Buf("t")], kind="d")
            for k in range(16):
                ps, PB = linear(ring, "in1", 49 + 2 * k, xr, xb, T)
                i = stg.next()
                S.add("scalar", act(stg.t[:, i, :], ps[:, 0:T], AF.Copy), reads=[PB], writes=[stg.b[i]])
                ps2, PB2 = linear(ring, "in1", 50 + 2 * k, xr, xb, T)
                i2 = stg.next()
                S.add("vector", tt(stg.t[:, i2, :], stg.t[:, i, :], ps2[:, 0:T], ALU.mult), reads=[stg.b[i], PB2], writes=[stg.b[i2]])
                S.add("scalar", dma(PCC[k, :, 2 + c0:2 + c0 + T], stg.t[:, i2, :]), reads=[stg.b[i2]], writes=[Buf("t")], kind="d")
            S.emit(block)

    def stage2(mt, full=True):
        c0 = mt * T
        with (sbt("s2_baT", [32, T], F32) as baT, sbt("s2_tm", [128, 7, 8, 16], F32) as tm,
              sbt("s2_pre", [128, 8, 3, 3 + T], F32) as pre, sbt("s2_qk", [128, 8, 3, T], F32) as qk,
              sbt("s2_cv", [128, T], F32) as cv, sbt("s2_sqb", [128, T], F32) as sqb,
              sbt("s2_rb", [128, T], F32) as rb, sbt("s2_yb", [128, 8, T], F32) as yb,
              sbt("s2_t64", [64, 8, 10, 64], F32) as t64, sbt("s2_t64w", [64, 8, 6, 128], F32) as t64w,
              sbt("s2_t128", [128, 8, 3, 64], F32) as t128, sbt("s2_ssq", [64, 8, 2], F32) as ssq,
              nc.Block() as block):
            BAT, TM, CV, SQB, RBB = Buf("bat"), Buf("tm"), Buf("cv"), Buf("sqb"), Buf("rb2")
            PREB = [Buf("pre%d" % i) for i in range(8)]
            HB = [Buf("hb%d" % i) for i in range(8)]
            YBB = [Buf("yb%d" % i) for i in range(8)]
            N64 = ["Gtri", "decs", "decT", "A0", "A1", "B0", "B1", "X0", "X1", "AQT"]
            N64W = ["KBG", "KT", "VB", "UB", "U", "ON"]
            N128 = ["egcb", "QDT", "WDT"]
            UB_ = [{n: Buf(n + str(s)) for n in N64 + N64W + N128 + ["ssq"]} for s in range(8)]
            G_, BETA, EGC, EKT, BEG, NB, EGL = range(7)

            def tmv(k, c, p=64):
                return tm[0:p, k, c, :]

            ONE64 = cst[0:64, C_ONE:C_ONE + 64]
            NEG64 = cst[0:64, C_NEG:C_NEG + 64]
            I64 = cst[0:64, C_I:C_I + 64]
            S.add("sync", dma(baT[:, :], BAp[:, c0:c0 + T]), writes=[BAT], kind="d")
            for c in range(8):
                ps, PB = bank()
                S.add("tensor", mm(ps[0:64, 0:32], baT[0:32, c * 64:(c + 1) * 64], cst[0:32, C_I:C_I + 32]), reads=[BAT, CST], writes=[PB])
                S.add("scalar", act(tmv(BETA, c), ps[0:64, 0:16], AF.Sigmoid), reads=[PB], writes=[TM])
                S.add("vector", tt(tmv(G_, c), ps[0:64, 16:32], vecs[0:64, V_DTB:V_DTB + 16], ALU.add), reads=[PB, VEC, TM], writes=[TM])
                S.add("scalar", act(tmv(G_, c), tmv(G_, c), AF.Exp), reads=[TM], writes=[TM])
                S.add("scalar", act(tmv(G_, c), tmv(G_, c), AF.Ln, bias=1.0), reads=[TM], writes=[TM])
                S.add("vector", tt(tmv(G_, c), tmv(G_, c), negA[0:64, :], ALU.mult), reads=[TM, NEGA], writes=[TM])
                ps2, PB2 = bank()
                S.add("tensor", mm(ps2[0:64, 0:16], cst[0:64, C_TRI:C_TRI + 64], tmv(G_, c)), reads=[TM, CST], writes=[PB2])
                S.add("tensor", mm(ps2[0:64, 16:32], cst[0:64, C_UPP:C_UPP + 64], tmv(G_, c)), reads=[TM, CST])
                S.add("tensor", mm(ps2[:, 32:48], cst[0:64, C_ONE:C_ONE + 128], tmv(G_, c)), reads=[TM, CST], writes=[PB2])
                S.add("scalar", act(tmv(EGC, c), ps2[0:64, 0:16], AF.Exp), reads=[PB2, TM], writes=[TM])
                S.add("scalar", act(tmv(EKT, c), ps2[0:64, 16:32], AF.Exp), reads=[PB2, TM], writes=[TM])
                S.add("scalar", act(tmv(EGL, c, 128), ps2[:, 32:48], AF.Exp), reads=[PB2, TM], writes=[TM])
                S.add("vector", tt(tmv(BEG, c), tmv(BETA, c), tmv(EGC, c), ALU.mult), reads=[TM], writes=[TM])
                S.add("vector", ts(tmv(NB, c), tmv(BETA, c), -1.0, None, ALU.mult), reads=[TM], writes=[TM])

            def prep(h, hs):
                for qi in range(3):
                    cidx = qi * 16 + h
                    S.add("sync", dma(pre[:, hs, qi, :], QKV[cidx, :, c0:c0 + 3 + T]), writes=[PREB[hs]], kind="d")
                for qi in range(3):
                    cidx = qi * 16 + h
                    wv = lambda i, cidx=cidx: vecs[:, V_CQ + cidx * 4 + i:V_CQ + cidx * 4 + i + 1]
                    S.add("vector", ts(cv[:, :], pre[:, hs, qi, 0:T], wv(0), None, ALU.mult), reads=[PREB[hs], VEC], writes=[CV])
                    for i in range(1, 4):
                        S.add("vector", stt(cv[:, :], pre[:, hs, qi, i:i + T], wv(i), cv[:, :], ALU.mult, ALU.add),
                              reads=[PREB[hs], VEC, CV], writes=[CV])
                    S.add("scalar", act(qk[:, hs, qi, :], cv[:, :], AF.Silu), reads=[CV], writes=[HB[hs]])
                for qi in (0, 1):
                    S.add("scalar", act(sqb[:, :], qk[:, hs, qi, :], AF.Square), reads=[HB[hs]], writes=[SQB])
                    ps, PB = bank()
                    S.add("tensor", mm(ps[:, 0:T], cst[:, C_ONE:C_ONE + 128], sqb[:, :]), reads=[SQB, CST], writes=[PB])
                    S.add("scalar", act(rb[:, :], ps[:, 0:T], AF.Sqrt, bias=EPS, scale=1.0), reads=[PB], writes=[RBB])
                    S.add("vector", recip(rb[:, :], rb[:, :]), reads=[RBB], writes=[RBB])
                    S.add("vector", stt(qk[:, hs, qi, :], qk[:, hs, qi, :], (128.0 ** -0.5) if qi == 0 else 1.0, rb[:, :], ALU.mult, ALU.mult),
                          reads=[HB[hs], RBB], writes=[HB[hs]])
            SH = [Buf("sst%d" % i) for i in range(NH)]

            def unit(h, c, hs, us):
                if True:
                    ub = UB_[us]
                    ub = UB_[us]
                    cs = slice(c * 64, (c + 1) * 64)
                    qT, kT, vT = qk[:, hs, 0, cs], qk[:, hs, 1, cs], qk[:, hs, 2, cs]
                    a64 = lambda n: t64[:, us, N64.index(n), :]
                    a64w = lambda n: t64w[:, us, N64W.index(n), :]
                    a128 = lambda n: t128[:, us, N128.index(n), :]
                    sc = lambda k: tm[0:64, k, c, h:h + 1]
                    S.add("vector", ts(a64("Gtri"), cst[0:64, C_TRI:C_TRI + 64], sc(G_), None, ALU.mult), reads=[CST, TM], writes=[ub["Gtri"]])
                    yield
                    ps, PB = bank()
                    S.add("tensor", mm(ps[0:64, 0:64], a64("Gtri"), ONE64, True, False), reads=[ub["Gtri"], CST], writes=[PB])
                    S.add("tensor", mm(ps[0:64, 0:64], NEG64, a64("Gtri"), False, False), reads=[ub["Gtri"], CST])
                    S.add("tensor", mm(ps[0:64, 0:64], I64, cst[0:64, C_NMS:C_NMS + 64], False, True), reads=[CST], writes=[PB])
                    S.add("scalar", act(a64("decs"), ps[0:64, 0:64], AF.Exp), reads=[PB], writes=[ub["decs"]])
                    if full:
                        yield
                        ps, PB = bank()
                        S.add("tensor", mm(ps[0:64, 0:64], ONE64, a64("Gtri"), True, False), reads=[ub["Gtri"], CST], writes=[PB])
                        S.add("tensor", mm(ps[0:64, 0:64], a64("Gtri"), NEG64, False, False), reads=[ub["Gtri"], CST])
                        S.add("tensor", mm(ps[0:64, 0:64], I64, cst[0:64, C_NMT:C_NMT + 64], False, True), reads=[CST], writes=[PB])
                        S.add("scalar", act(a64("decT"), ps[0:64, 0:64], AF.Exp), reads=[PB], writes=[ub["decT"]])
                        yield
                        ps, PB = bank()
                        S.add("tensor", mm(ps[:, 0:64], cst[0:64, C_ONE:C_ONE + 128], a64("Gtri")), reads=[ub["Gtri"], CST], writes=[PB])
                        S.add("scalar", act(a128("egcb"), ps[:, 0:64], AF.Exp), reads=[PB], writes=[ub["egcb"]])
                        S.add("vector", tt(a128("QDT"), qT, a128("egcb"), ALU.mult), reads=[HB[hs], ub["egcb"]], writes=[ub["QDT"]])
                    yield
                    ps, PB = bank()
                    S.add("tensor", mm(ps[0:64, 0:64], kT, kT), reads=[HB[hs]], writes=[PB])
                    S.add("vector", stt(a64("A0"), ps[0:64, 0:64], sc(NB), a64("decs"), ALU.mult, ALU.mult), reads=[PB, TM, ub["decs"]], writes=[ub["A0"]])
                    if full:
                        yield
                        ps, PB = bank()
                        S.add("tensor", mm(ps[0:64, 0:64], kT, qT), reads=[HB[hs]], writes=[PB])
                        S.add("vector", tt(a64("AQT"), ps[0:64, 0:64], a64("decT"), ALU.mult), reads=[PB, ub["decT"]], writes=[ub["AQT"]])
                    yield
                    ps, PB = bank()
                    S.add("tensor", mm(ps[0:64, 0:64], a64("A0"), I64), reads=[ub["A0"], CST], writes=[PB])
                    S.add("scalar", act(a64("B0"), ps[0:64, 0:64], AF.Copy), reads=[PB], writes=[ub["B0"]])
                    S.add("vector", tt(a64("X0"), ps[0:64, 0:64], I64, ALU.add), reads=[PB, CST], writes=[ub["X0"]])
                    for lvl in range(1, 6):
                        cu, nx = str((lvl - 1) % 2), str(lvl % 2)
                        yield
                        ps, PB = bank()
                        S.add("tensor", mm(ps[0:64, 0:64], a64("B" + cu), a64("A" + cu)), reads=[ub["A" + cu], ub["B" + cu]], writes=[PB])
                        S.add("scalar", act(a64("A" + nx), ps[0:64, 0:64], AF.Copy), reads=[PB], writes=[ub["A" + nx]])
                        if lvl < 5:
                            yield
                            ps, PB = bank()
                            S.add("tensor", mm(ps[0:64, 0:64], a64("A" + cu), a64("B" + cu)), reads=[ub["A" + cu], ub["B" + cu]], writes=[PB])
                            S.add("vector", cp(a64("B" + nx), ps[0:64, 0:64]), reads=[PB], writes=[ub["B" + nx]])
                        yield
                        ps, PB = bank()
                        S.add("tensor", mm(ps[0:64, 0:64], a64("A" + nx), a64("X" + cu)), reads=[ub["A" + nx], ub["X" + cu]], writes=[PB])
                        S.add("vector", tt(a64("X" + nx), a64("X" + cu), ps[0:64, 0:64], ALU.add), reads=[PB, ub["X" + cu]], writes=[ub["X" + nx]])
                    XT, XB = a64("X1"), ub["X1"]
                    yield
                    ps, PB = bank()
                    S.add("tensor", mm(ps[0:64, 0:128], kT, I128()), reads=[HB[hs], CST], writes=[PB])
                    S.add("scalar", act(a64w("KBG"), ps[0:64, 0:128], AF.Copy, scale=sc(BEG)), reads=[PB, TM], writes=[ub["KBG"]])
                    S.add("vector", ts(a64w("KT"), ps[0:64, 0:128], sc(EKT), None, ALU.mult), reads=[PB, TM], writes=[ub["KT"]])
                    yield
                    ps, PB = bank()
                    S.add("tensor", mm(ps[0:64, 0:128], vT, I128()), reads=[HB[hs], CST], writes=[PB])
                    S.add("scalar", act(a64w("VB"), ps[0:64, 0:128], AF.Copy, scale=sc(BETA)), reads=[PB, TM], writes=[ub["VB"]])
                    yield
                    ps, PB = bank()
                    S.add("tensor", mm(ps[:, 0:64], a64w("KBG"), XT), reads=[ub["KBG"], XB], writes=[PB])
                    S.add("scalar", act(a128("WDT"), ps[:, 0:64], AF.Copy), reads=[PB], writes=[ub["WDT"]])
                    yield
                    ps, PB = bank()
                    S.add("tensor", mm(ps[0:64, 0:128], XT, a64w("VB")), reads=[ub["VB"], XB], writes=[PB])
                    S.add("vector", cp(a64w("UB"), ps[0:64, 0:128]), reads=[PB], writes=[ub["UB"]])
                    yield
                    ps, PB = bank()
                    S.add("tensor", mm(ps[0:64, 0:128], a128("WDT"), Sst[:, h, :]), reads=[ub["WDT"], SH[h]], writes=[PB])
                    S.add("vector", tt(a64w("U"), a64w("UB"), ps[0:64, 0:128], ALU.subtract), reads=[PB, ub["UB"]], writes=[ub["U"]])
                    if full:
                        yield
                        ps, PB = bank()
                        S.add("tensor", mm(ps[0:64, 0:128], a128("QDT"), Sst[:, h, :], True, False), reads=[ub["QDT"], SH[h]], writes=[PB])
                        S.add("tensor", mm(ps[0:64, 0:128], a64("AQT"), a64w("U"), False, True), reads=[ub["AQT"], ub["U"]], writes=[PB])
                        S.add("scalar", act(a64w("ON"), ps[0:64, 0:128], AF.Square, accum_out=ssq[:, us, 0:1]), reads=[PB], writes=[ub["ON"], ub["ssq"]])
                        S.add("scalar", act(ssq[:, us, 1:2], ssq[:, us, 0:1], AF.Sqrt, bias=EPS, scale=1.0 / 128), reads=[ub["ssq"]], writes=[ub["ssq"]])
                        S.add("vector", recip(ssq[:, us, 1:2], ssq[:, us, 1:2]), reads=[ub["ssq"]], writes=[ub["ssq"]])
                        S.add("vector", ts(a64w("ON"), ps[0:64, 0:128], ssq[:, us, 1:2], None, ALU.mult), reads=[PB, ub["ssq"], ub["ON"]], writes=[ub["ON"]])
                    yield
                    ps2, PB2 = bank()
                    S.add("tensor", mm(ps2[:, 0:128], a64w("KT"), a64w("U")), reads=[ub["KT"], ub["U"]], writes=[PB2])
                    S.add("vector", stt(Sst[:, h, :], Sst[:, h, :], tm[:, EGL, c, h:h + 1], ps2[:, 0:128], ALU.mult, ALU.add),
                          reads=[PB2, TM, SH[h]], writes=[SH[h]])
                    if full:
                        yield
                        ps, PB = bank()
                        S.add("tensor", mm(ps[:, 0:64], a64w("ON"), I64), reads=[ub["ON"], CST], writes=[PB])
                        S.add("scalar", act(yb[:, hs, cs], ps[:, 0:64], AF.Copy, scale=vecs[:, V_DNG:V_DNG + 1]), reads=[PB, VEC], writes=[YBB[hs]])


            G = 8
            for hg in range(0, NH, G):
                for h in range(hg, hg + G):
                    prep(h, h - hg)
                for c in range(8):
                    gens = [unit(h, c, h - hg, h - hg) for h in range(hg, hg + G)]
                    while gens:
                        for g_ in list(gens):
                            try:
                                next(g_)
                            except StopIteration:
                                gens.remove(g_)
                for h in range(hg, hg + G):
                    hs = h - hg
                    if full:
                        S.add("scalar", dma(YB[h, :, c0:c0 + T], yb[:, hs, :]), reads=[YBB[hs]], writes=[Buf("t")], kind="d")
            S.emit(block)

    def stageM():
        with (sbt("sm_mem", [128, 32, 256], F32) as memT, sbt("sm_mn", [128, 32, 256], BF16) as mn,
              sbt("sm_ring", [128, NSLOT, 32, 128], BF16) as ringt, sbt("sm_sq", [128, 2, 256], F32) as sqt,
              sbt("sm_rb", [128, 256], F32) as rb, sbt("sm_vmT", [128, 4, 256], BF16) as vmT,
              nc.Block() as block):
            MB = [Buf("mem%d" % k) for k in range(32)]
            MNB = [Buf("mn%d" % k) for k in range(32)]
            RBB, VMT = Buf("rbm"), Buf("vmT")
            ring, sq = Ring(ringt, NSLOT, "ringm"), Ring(sqt, 2, "sqm")
            for kc in range(32):
                S.add("sync", dma(memT[:, kc, :], memT_d[:, kc, :]), writes=[MB[kc]], kind="d")
            rmsnorm(lambda kc: memT[:, kc, :], lambda kc: MB[kc], 3, lambda kc: mn[:, kc, :], lambda kc: MNB[kc], 256, sq, rb, RBB)
            for c in range(4):
                ps, PB = linear(ring, "xa", 4 + c, lambda kc: mn[:, kc, :], lambda kc: MNB[kc], 256)
                S.add("scalar", act(KmT[:, c, :], ps[:, 0:256], AF.Copy), reads=[PB], writes=[KMT])
            for c in range(4):
                ps, PB = linear(ring, "xa", 8 + c, lambda kc: mn[:, kc, :], lambda kc: MNB[kc], 256)
                S.add("scalar", act(vmT[:, c, :], ps[:, 0:256], AF.Copy), reads=[PB], writes=[VMT])
            for c in range(4):
                for mc in range(2):
                    ps, PB = bank()
                    S.add("tensor", mm(ps[:, 0:128], vmT[:, c, mc * 128:(mc + 1) * 128], ibf[:, :]), reads=[VMT, IBF], writes=[PB])
                    S.add("vector", cp(Vm[:, mc, c * 128:(c + 1) * 128], ps[:, 0:128]), reads=[PB], writes=[VM])
            S.emit(block)

    def stage4a(mt):
        c0 = mt * T
        with (sbt("a_u", [128, 32, T], BF16) as uT, sbt("a_ring", [128, NSLOT, 32, 128], BF16) as ringt,
              sbt("a_yA", [128, 16, T], BF16) as yA, sbt("a_yB", [128, 16, T], BF16) as yB,
              sbt("a_mrg", [128, 32, T], BF16) as mrg, sbt("a_pcc", [128, 2, 2 + T], F32) as pcc,
              sbt("a_cv", [128, T], F32) as cv, sbt("a_ybl", [128, 2, T], F32) as ybl,
              sbt("a_sz", [128, 2, T], F32) as szt, sbt("a_m", [128, 2, T], F32) as mt_,
              sbt("a_h", [128, 2, T], F32) as hch, nc.Block() as block):
            UBF = [Buf("u%d" % k) for k in range(32)]
            YAB = [Buf("ya%d" % k) for k in range(16)]
            YBB = [Buf("yb%d" % k) for k in range(16)]
            MRB = [Buf("mr%d" % k) for k in range(32)]
            CV = Buf("cva")
            ring = Ring(ringt, NSLOT, "ringa")
            pc, yl, sz, mm_, hc = Ring(pcc, 2, "pc"), Ring(ybl, 2, "yl"), Ring(szt, 2, "sz"), Ring(mt_, 2, "m"), Ring(hch, 2, "hc")
            for kc in range(32):
                S.add("sync", dma(uT[:, kc, :], U1[:, kc, c0:c0 + T]), writes=[UBF[kc]], kind="d")
            ur, ubf = (lambda kc: uT[:, kc, :]), (lambda kc: UBF[kc])
            for i in range(16):
                ps, PB = linear(ring, "in4", i, ur, ubf, T)
                k = pc.next()
                S.add("sync", dma(pcc[:, k, :], PCC[i, :, c0:c0 + 2 + T]), writes=[pc.b[k]], kind="d")
                wv = lambda j, i=i: vecs[:, V_CA + i * 3 + j:V_CA + i * 3 + j + 1]
                S.add("vector", ts(cv[:, :], pcc[:, k, 0:T], wv(0), None, ALU.mult), reads=[pc.b[k], VEC], writes=[CV])
                for j in (1, 2):
                    S.add("vector", stt(cv[:, :], pcc[:, k, j:j + T], wv(j), cv[:, :], ALU.mult, ALU.add), reads=[pc.b[k], VEC, CV], writes=[CV])
                S.add("vector", tt(yA[:, i, :], cv[:, :], ps[:, 0:T], ALU.mult), reads=[CV, PB], writes=[YAB[i]])
            for h in range(16):
                ps, PB = linear(ring, "in4", 16 + h, ur, ubf, T)
                k = sz.next()
                S.add("scalar", act(szt[:, k, :], ps[:, 0:T], AF.Silu), reads=[PB], writes=[sz.b[k]])
                k2 = yl.next()
                S.add("sync", dma(ybl[:, k2, :], YB[h, :, c0:c0 + T]), writes=[yl.b[k2]], kind="d")
                S.add("vector", tt(yB[:, h, :], ybl[:, k2, :], szt[:, k, :], ALU.mult), reads=[yl.b[k2], sz.b[k]], writes=[YBB[h]])
            for oc in range(32):
                ps, PB = linear(ring, "in4", 32 + oc, ur, ubf, T)
                k = sz.next()
                S.add("scalar", act(szt[:, k, :], ps[:, 0:T], AF.Sigmoid), reads=[PB], writes=[sz.b[k]])
                ps, PB = linear(ring, "oc", oc, lambda kc: yA[:, kc, :], lambda kc: YAB[kc], T)
                ka = mm_.next()
                S.add("vector", tt(mt_[:, ka, :], szt[:, k, :], ps[:, 0:T], ALU.mult), reads=[sz.b[k], PB], writes=[mm_.b[ka]])
                ps, PB = linear(ring, "in4", 64 + oc, ur, ubf, T)
                k = sz.next()
                S.add("scalar", act(szt[:, k, :], ps[:, 0:T], AF.Sigmoid), reads=[PB], writes=[sz.b[k]])
                ps, PB = linear(ring, "od", oc, lambda kc: yB[:, kc, :], lambda kc: YBB[kc], T)
                kb = mm_.next()
                S.add("vector", tt(mt_[:, kb, :], szt[:, k, :], ps[:, 0:T], ALU.mult), reads=[sz.b[k], PB], writes=[mm_.b[kb]])
                S.add("vector", tt(mrg[:, oc, :], mt_[:, ka, :], mt_[:, kb, :], ALU.add), reads=[mm_.b[ka], mm_.b[kb]], writes=[MRB[oc]])
            for oc in range(32):
                ps, PB = linear(ring, "wo", oc, lambda kc: mrg[:, kc, :], lambda kc: MRB[kc], T)
                k = hc.next()
                S.add("sync", dma(hch[:, k, :], H1[:, oc, c0:c0 + T]), writes=[hc.b[k]], kind="d")
                S.add("vector", tt(hch[:, k, :], hch[:, k, :], ps[:, 0:T], ALU.add), reads=[hc.b[k], PB], writes=[hc.b[k]])
                S.add("scalar", dma(H1[:, oc, c0:c0 + T], hch[:, k, :]), reads=[hc.b[k]], writes=[Buf("t")], kind="d")
            S.emit(block)

    def stage4b(mt):
        c0 = mt * T
        SC = 128.0 ** -0.5
        with (sbt("b_hT", [128, 32, T], F32) as hT, sbt("b_xn", [128, 32, T], BF16) as xn,
              sbt("b_hid", [128, 22, T], BF16) as hid, sbt("b_ring", [128, 4, 32, 128], BF16) as ringt,
              sbt("b_sq", [128, 2, T], F32) as sqt, sbt("b_sg", [128, 2, T], F32) as sgt,
              sbt("b_rb", [128, T], F32) as rb, sbt("b_stg", [128, 2, T], F32) as stgt,
              sbt("b_qx", [128, 4, T], BF16) as qx, sbt("b_ox", [128, 4, T], BF16) as ox,
              sbt("b_pf", [128, 2, 256], F32) as pft, sbt("b_pn", [128, 2, 256], BF16) as pnt,
              sbt("b_pT", [128, 2, 2, 128], BF16) as pTt, sbt("b_st", [128, 2, 4], F32) as stt_,
              nc.Block() as block):
            HTB = [Buf("hT%d" % k) for k in range(32)]
            XNB = [Buf("xn%d" % k) for k in range(32)]
            HIDB = [Buf("hid%d" % k) for k in range(22)]
            QXB = [Buf("qx%d" % k) for k in range(4)]
            OXB = [Buf("ox%d" % k) for k in range(4)]
            RBB = Buf("rbb")
            ring = Ring(ringt, 4, "ringb")
            sq, sg, stg = Ring(sqt, 2, "sq"), Ring(sgt, 2, "sg"), Ring(stgt, 2, "stg")
            pf, pn, pT, st = Ring(pft, 2, "pf"), Ring(pnt, 2, "pn"), Ring(pTt, 2, "pT"), Ring(stt_, 2, "st")
            for kc in range(32):
                S.add("sync", dma(hT[:, kc, :], H1[:, kc, c0:c0 + T]), writes=[HTB[kc]], kind="d")
            hr, hb = (lambda kc: hT[:, kc, :]), (lambda kc: HTB[kc])
            xr, xb = (lambda kc: xn[:, kc, :]), (lambda kc: XNB[kc])
            rmsnorm(hr, hb, 2, xr, xb, T, sq, rb, RBB)
            for c in range(4):
                ps, PB = linear(ring, "xa", c, xr, xb, T)
                S.add("scalar", act(qx[:, c, :], ps[:, 0:T], AF.Copy), reads=[PB], writes=[QXB[c]])
            for hh in range(4):
                for tq in range(4):
                    tsl = slice(tq * 128, (tq + 1) * 128)
                    ps, PB = bank()
                    S.add("tensor", mm(ps[:, 0:256], qx[:, hh, tsl], KmT[:, hh, :]), reads=[QXB[hh], KMT], writes=[PB])
                    k = st.next()
                    S.add("vector", lambda e, ps=ps, k=k: e.tensor_reduce(out=stt_[:, k, 0:1], in_=ps[:, 0:256], axis=mybir.AxisListType.X, op=ALU.max),
                          reads=[PB], writes=[st.b[k]])
                    S.add("vector", ts(stt_[:, k, 1:2], stt_[:, k, 0:1], -SC, None, ALU.mult), reads=[st.b[k]], writes=[st.b[k]])
                    kf = pf.next()
                    S.add("scalar", act(pft[:, kf, :], ps[:, 0:256], AF.Exp, bias=stt_[:, k, 1:2], scale=SC, accum_out=stt_[:, k, 2:3]),
                          reads=[PB, st.b[k]], writes=[pf.b[kf], st.b[k]])
                    S.add("vector", recip(stt_[:, k, 3:4], stt_[:, k, 2:3]), reads=[st.b[k]], writes=[st.b[k]])
                    kn = pn.next()
                    S.add("vector", ts(pnt[:, kn, :], pft[:, kf, :], stt_[:, k, 3:4], None, ALU.mult), reads=[pf.b[kf], st.b[k]], writes=[pn.b[kn]])
                    kt = pT.next()
                    for mc in range(2):
                        ps2, PB2 = bank()
                        S.add("tensor", mm(ps2[:, 0:128], pnt[:, kn, mc * 128:(mc + 1) * 128], ibf[:, :]), reads=[pn.b[kn], IBF], writes=[PB2])
                        S.add("scalar", act(pTt[:, kt, mc, :], ps2[:, 0:128], AF.Copy), reads=[PB2], writes=[pT.b[kt]])
                    ps3, PB3 = bank()
                    S.add("tensor", mm(ps3[:, 0:128], Vm[:, 0, hh * 128:(hh + 1) * 128], pTt[:, kt, 0, :], True, False), reads=[VM, pT.b[kt]], writes=[PB3])
                    S.add("tensor", mm(ps3[:, 0:128], Vm[:, 1, hh * 128:(hh + 1) * 128], pTt[:, kt, 1, :], False, True), reads=[VM, pT.b[kt]], writes=[PB3])
                    S.add("vector", cp(ox[:, hh, tsl], ps3[:, 0:128]), reads=[PB3], writes=[OXB[hh]])
            for oc in range(32):
                ps, PB = linear(ring, "xo", oc, lambda kc: ox[:, kc, :], lambda kc: OXB[kc], T)
                S.add("vector", tt(hT[:, oc, :], hT[:, oc, :], ps[:, 0:T], ALU.add), reads=[PB, HTB[oc]], writes=[HTB[oc]])
            rmsnorm(hr, hb, 4, xr, xb, T, sq, rb, RBB)
            ffn(ring, "f2", xn, XNB, hT, HTB, hid, HIDB, sg)
            SB_ = [None]

            def fdst(kc):
                SB_[0] = stg.next()
                return stgt[:, SB_[0], :]

            ps, PB = bank()
            for kc in range(32):
                i = sq.next()
                S.add("scalar", act(sqt[:, i, :], hT[:, kc, :], AF.Square), reads=[HTB[kc]], writes=[sq.b[i]])
                S.add("tensor", mm(ps[:, 0:T], cst[:, C_ONE:C_ONE + 128], sqt[:, i, :], kc == 0, kc == 31),
                      reads=[sq.b[i], CST], writes=[PB] if kc in (0, 31) else [])
            S.add("scalar", act(rb[:, :], ps[:, 0:T], AF.Sqrt, bias=EPS, scale=1.0 / D), reads=[PB], writes=[RBB])
            S.add("vector", recip(rb[:, :], rb[:, :]), reads=[RBB], writes=[RBB])
            for kc in range(32):
                i = stg.next()
                S.add("vector", stt(stgt[:, i, :], hT[:, kc, :], gain(5, kc), rb[:, :], ALU.mult, ALU.mult),
                      reads=[HTB[kc], RBB, VEC], writes=[stg.b[i]])
                S.add("scalar", dma(yT_d[:, kc, c0:c0 + T], stgt[:, i, :]), reads=[stg.b[i]], writes=[Buf("t")], kind="d")
            S.emit(block)

    PAIRS = [list(range(8))]
    XI1 = nc.dram_tensor("XI1", [128, 176], F32, kind="Internal").ap()
    XO1 = nc.dram_tensor("XO1", [8 * 128, 176], F32, kind="Internal").ap()
    XI2 = nc.dram_tensor("XI2", [128, 2048], F32, kind="Internal").ap()
    XO2 = nc.dram_tensor("XO2", [8 * 128, 2048], F32, kind="Internal").ap()
    def oh(r):
        return vecs[:, V_OH + r:V_OH + r + 1]

    def exch1():
        tl = nmt * T
        with (sbt("x1a", [128, 176], F32) as hx, sbt("x1b", [128, 176], F32) as hx2,
              sbt("x1c", [128, 8, 176], F32) as hx8, nc.Block() as block):
            HX, HX2, BXI, BXO = Buf("hx"), Buf("hx2"), Buf("xi1"), Buf("xo1")
            for c in range(48):
                S.add("sync", dma(hx[:, 3 * c:3 * c + 3], QKV[c, :, tl:tl + 3]), writes=[HX], reads=[], kind="d")
                HX.writer = None
            for c in range(16):
                S.add("sync", dma(hx[:, 144 + 2 * c:146 + 2 * c], PCC[c, :, tl:tl + 2]), writes=[HX], reads=[], kind="d")
                HX.writer = None
            loads = S.ops["sync"][-64:]
            st_ = S.add("sync", dma(XI1, hx[:, :]), reads=[], writes=[BXI], kind="d")
            for d_ in loads:
                d_.signal = True
                st_.deps.append(d_)
            S.add("gpsimd", lambda e: e.collective_compute("AllGather", ALU.bypass, replica_groups=PAIRS, ins=[XI1], outs=[XO1]),
                  reads=[BXI], writes=[BXO], kind="cc")
            HX8 = Buf("hx8")
            S.add("sync", dma(hx8[:, :, :], XO1.rearrange("(r p) n -> p r n", p=128)), reads=[BXO], writes=[HX8], kind="d")
            S.add("vector", ts(hx2[:, :], hx8[:, 0, :], oh(0), None, ALU.mult), reads=[HX8, VEC], writes=[HX2])
            for r in range(1, 8):
                S.add("vector", stt(hx2[:, :], hx8[:, r, :], oh(r), hx2[:, :], ALU.mult, ALU.add), reads=[HX8, VEC, HX2], writes=[HX2])
            for c in range(48):
                S.add("sync", dma(QKV[c, :, 0:3], hx2[:, 3 * c:3 * c + 3]), reads=[HX2], writes=[Buf("t")], kind="d")
            for c in range(16):
                S.add("sync", dma(PCC[c, :, 0:2], hx2[:, 144 + 2 * c:146 + 2 * c]), reads=[HX2], writes=[Buf("t")], kind="d")
            S.emit(block)

    def exch2():
        with sbt("x2", [128, 8, 2048], F32) as sx, nc.Block() as block:
            SX, BXI, BXO = Buf("sx"), Buf("xi2"), Buf("xo2")
            S.add("sync", dma(XI2, Sst[:].rearrange("p a b -> p (a b)")), reads=[SST], writes=[BXI], kind="d")
            S.add("gpsimd", lambda e: e.collective_compute("AllGather", ALU.bypass, replica_groups=PAIRS, ins=[XI2], outs=[XO2]),
                  reads=[BXI], writes=[BXO], kind="cc")
            for r in range(8):
                S.add("sync", dma(sx[:, r, :], XO2[r * 128:(r + 1) * 128, :]), reads=[BXO], writes=[SX], kind="d")
                SX.writer = None
            lds = S.ops["sync"][-8:]
            sflat = Sst[:].rearrange("p a b -> p (a b)")
            op = S.add("vector", ts(sflat, sx[:, 0, :], oh(0), None, ALU.mult), reads=[VEC, SST], writes=[SST])
            for d_ in lds:
                d_.signal = True
                op.deps.append(d_)
            for r in range(1, 8):
                S.add("vector", stt(sflat, sx[:, r, :], oh(r), sflat, ALU.mult, ALU.add), reads=[VEC, SST], writes=[SST])
            S.emit(block)

    if "1" in stages:
        for mt in range(nmt):
            stage1(mt)
    if "2" in stages:
        if "x" in stages:
            exch1()
        for mt in range(nmt):
            stage2(mt, full=("x" not in stages))
        if "x" in stages:
            exch2()
            for mt in range(nmt):
                stage2(mt)
    if "M" in stages:
        stageM()
    for mt in range(nmt):
        if "4a" in stages:
            stage4a(mt)
        if "4b" in stages:
            stage4b(mt)
    if debug:
        with sbt("dbg", [128, 4096], F32) as dbg, nc.Block() as block:
            DB_ = Buf("dd")
            S.add("vector", cp(dbg[:, 0:1024], KmT[:].rearrange("p a b -> p (a b)")), writes=[DB_])
            S.add("vector", cp(dbg[:, 1024:2048], Vm[:].rearrange("p a b -> p (a b)")), writes=[DB_])
            S.add("vector", cp(dbg[:, 2048:4096], Sst[:, 0:16, :].rearrange("p a b -> p (a b)")), writes=[DB_])
            S.add("sync", dma(DBG, dbg[:]), reads=[DB_], writes=[Buf("d2")], kind="d")
            S.emit(block)
    return nc


_CACHE = {}


def kernel(**inputs):
    inp = {k: np.asarray(v) for k, v in inputs.items()}
    shards = _prep_weights(inp)
    vecs, consts = _prep_small(inp)
    x = inp["x"].astype(np.float32)
    mem = inp["mem"].astype(np.float32)
    in_maps = []
    for c in range(8):
        b, half = c // 2, c % 2
        m = dict(shards[c])
        xs = x[b, half * NTOK:(half + 1) * NTOK]
        m["xT"] = np.ascontiguousarray(xs.T.reshape(32, 128, NTOK).transpose(1, 0, 2))
        m["memT"] = np.ascontiguousarray(mem[b].T.reshape(32, 128, 256).transpose(1, 0, 2))
        vc = vecs.copy()
        if half == 1:
            vc[:, V_OH + c - 1] = 1.0
        m["vecs"] = vc
        m["consts"] = consts
        in_maps.append(m)
    if "nc" not in _CACHE:
        _CACHE["nc"] = build()
    res = run_bass_kernel_spmd(_CACHE["nc"], in_maps, core_ids=list(range(8)))
    out = np.empty((4, 2 * NTOK, D), np.float32)
    for c in range(8):
        b, half = c // 2, c % 2
        yT = np.asarray(res.results[c]["yT"])
        out[b, half * NTOK:(half + 1) * NTOK] = yT.transpose(2, 1, 0).reshape(NTOK, D)
    return out
```
